# Optimizing a Trainium2 kernel written in Bass

```python
import jax, jax.numpy as jnp
from jax import lax
import numpy as np

D_MODEL = 1024
BATCH = 4
SEQ = 4096
DEPTH = 1
DEC_BATCH = 128
DEC_SEQ = 8
PAST_LEN = 8192
PAGE_SIZE = 128

N_MEM = 256
MLA_HEADS = 8
QK_NOPE = 64
QK_ROPE = 32
V_HEAD = 64
Q_LORA = 256
KV_LORA = 128
MLA_W = MLA_HEADS * V_HEAD
MLA_SCALE = (QK_NOPE + QK_ROPE) ** -0.5
ROPE_BASE = 10000.0
RWKV_HEADS = 4
RWKV_N = 64
RWKV_W = RWKV_HEADS * RWKV_N
DECAY_LORA = 64
AAA_LORA = 64
SHIFT_COLS = 3 * RWKV_W + DECAY_LORA + AAA_LORA
GN_EPS = 64e-5
X_HEADS = 4
X_HEAD_DIM = 64
X_W = X_HEADS * X_HEAD_DIM
X_SCALE = X_HEAD_DIM ** -0.5
MIX_W = MLA_W + RWKV_W + X_W
IN_SPLITS = (Q_LORA, KV_LORA, QK_ROPE, MLA_W, SHIFT_COLS, RWKV_W, X_W, X_W)
IN_COLS = Q_LORA + KV_LORA + QK_ROPE + MLA_W + SHIFT_COLS + RWKV_W + X_W + X_W
Q_BLOCK = 128
RMS_EPS = 1e-6
NEG_INF = -1e30

kernel_name = 'hymba_mla_rwkv7_memory_decoder_step'


def rmsnorm(x, g):
    xf = x.astype(jnp.float32)
    y = xf * lax.rsqrt(jnp.mean(jnp.square(xf), axis=-1, keepdims=True) + RMS_EPS)
    return (y * g.astype(jnp.float32)).astype(x.dtype)


def rope(x, pos):
    r = x.shape[-1]
    half = r // 2
    inv = 1.0 / (ROPE_BASE ** (jnp.arange(half, dtype=jnp.float32) * (2.0 / r)))
    ang = pos.astype(jnp.float32)[:, None] * inv[None, :]
    ang = ang.reshape((1, ang.shape[0]) + (1,) * (x.ndim - 3) + (half,))
    c, s = jnp.cos(ang), jnp.sin(ang)
    xf = x.astype(jnp.float32)
    x1, x2 = xf[..., :half], xf[..., half:]
    return jnp.concatenate([x1 * c - x2 * s, x2 * c + x1 * s], axis=-1).astype(x.dtype)


def mixer_inputs(x, pos, ln_g, w_in, q_norm_g, kv_norm_g, w_uq):
    B, T, _ = x.shape
    h = rmsnorm(x, ln_g)
    proj = jnp.einsum('btd,dc->btc', h, w_in)
    c_q, c_kv, k_r, g_mla, z, g_rwkv, q_x, g_x = jnp.split(proj, np.cumsum(IN_SPLITS)[:-1], axis=-1)
    q = jnp.einsum('btc,chd->bthd', rmsnorm(c_q, q_norm_g), w_uq)
    q_nope = q[..., :QK_NOPE]
    q_pe = rope(q[..., QK_NOPE:], pos)
    c_kv = rmsnorm(c_kv, kv_norm_g)
    k_pe = rope(k_r, pos)
    q_x = q_x.reshape(B, T, X_HEADS, X_HEAD_DIM)
    return q_nope, q_pe, c_kv, k_pe, g_mla, z, g_rwkv, q_x, g_x


def mla_prompt_attention(q_nope, q_pe, k_nope, k_pe, v):
    B, S, H, _ = q_nope.shape
    nb = S // Q_BLOCK
    blk = lambda t: jnp.moveaxis(t.reshape((B, nb, Q_BLOCK) + t.shape[2:]), 1, 0)
    kpos = jnp.arange(S)

    def one_block(args):
        i, qn, qp = args
        s = jnp.einsum('bqhd,bkhd->bhqk', qn, k_nope) + jnp.einsum('bqhr,bkr->bhqk', qp, k_pe)
        qpos = i * Q_BLOCK + jnp.arange(Q_BLOCK)
        s = jnp.where(kpos[None, :] <= qpos[:, None], s.astype(jnp.float32) * MLA_SCALE, NEG_INF)
        p = jax.nn.softmax(s, axis=-1).astype(v.dtype)
        return jnp.einsum('bhqk,bkhd->bqhd', p, v)

    o = lax.map(one_block, (jnp.arange(nb), blk(q_nope), blk(q_pe)))
    return jnp.moveaxis(o, 0, 1).reshape(B, S, H * V_HEAD)


def mla_sample_attention(q_nope, q_pe, c_kv, k_pe, ckv_past, kpe_past, w_uk, w_uv):
    B, T, H, _ = q_nope.shape
    P = ckv_past.shape[1]
    q_lat = jnp.einsum('bthd,chd->bthc', q_nope, w_uk)
    s_past = jnp.einsum('bthc,bkc->bhtk', q_lat, ckv_past) + jnp.einsum('bthr,bkr->bhtk', q_pe, kpe_past)
    s_new = jnp.einsum('bthc,buc->bhtu', q_lat, c_kv) + jnp.einsum('bthr,bur->bhtu', q_pe, k_pe)
    causal = jnp.tril(jnp.ones((T, T), dtype=bool))
    s = jnp.concatenate([s_past.astype(jnp.float32) * MLA_SCALE,
                         jnp.where(causal, s_new.astype(jnp.float32) * MLA_SCALE, NEG_INF)], axis=-1)
    p = jax.nn.softmax(s, axis=-1).astype(c_kv.dtype)
    o_lat = jnp.einsum('bhtk,bkc->bthc', p[..., :P], ckv_past) + jnp.einsum('bhtu,buc->bthc', p[..., P:], c_kv)
    return jnp.einsum('bthc,chd->bthd', o_lat, w_uv).reshape(B, T, H * V_HEAD)


def wkv_step(S, inp):
    r_t, w_t, k_t, v_t, kk_t, a_t = inp
    sa = jnp.einsum('bhvk,bhk->bhv', S, -kk_t)
    S = S * w_t[:, :, None, :] + sa[..., None] * (kk_t * a_t)[:, :, None, :] + v_t[..., None] * k_t[:, :, None, :]
    y = jnp.einsum('bhvk,bhk->bhv', S, r_t)
    return S, y


def rwkv7_branch(z, prev_row, wkv0, mu, w0, w_up, a0, a_up, k_k, k_a, r_k, lnx_g, lnx_b):
    B, T, _ = z.shape
    f32 = jnp.float32
    prev = jnp.concatenate([prev_row[:, None, :].astype(z.dtype), z[:, :-1]], axis=1)
    zm = z + (prev - z) * mu
    r, k, v, wd, ad = jnp.split(zm, [RWKV_W, 2 * RWKV_W, 3 * RWKV_W, 3 * RWKV_W + DECAY_LORA], axis=-1)
    w = -jax.nn.softplus(-(w0 + jnp.tanh(wd) @ w_up).astype(f32)) - 0.5
    decay = jnp.exp(-jnp.exp(w))
    a = jax.nn.sigmoid((a0 + ad @ a_up).astype(f32))
    heads = lambda t: t.reshape(B, T, RWKV_HEADS, RWKV_N)
    kf = k.astype(f32)
    kk = heads(kf * k_k.astype(f32))
    kk = kk / jnp.maximum(jnp.sqrt(jnp.sum(jnp.square(kk), axis=-1, keepdims=True)), 1e-12)
    kf = heads(kf * (1.0 + (a - 1.0) * k_a.astype(f32)))
    rf, vf, decay, a = heads(r.astype(f32)), heads(v.astype(f32)), heads(decay), heads(a)
    tm = lambda t: jnp.swapaxes(t, 0, 1)
    S_fin, ys = lax.scan(wkv_step, wkv0.astype(f32), (tm(rf), tm(decay), tm(kf), tm(vf), tm(kk), tm(a)))
    y = tm(ys)
    mean = jnp.mean(y, axis=-1, keepdims=True)
    var = jnp.mean(jnp.square(y - mean), axis=-1, keepdims=True)
    y = ((y - mean) * lax.rsqrt(var + GN_EPS)).reshape(B, T, RWKV_W) * lnx_g.astype(f32) + lnx_b.astype(f32)
    bonus = jnp.sum(rf * kf * r_k.astype(f32), axis=-1, keepdims=True) * vf
    out = y + bonus.reshape(B, T, RWKV_W)
    return out.astype(z.dtype), S_fin.astype(z.dtype), z[:, -1]


def memory_kv(mem, g, w):
    kv = jnp.einsum('bmd,dchk->cbmhk', rmsnorm(mem, g), w)
    return kv[0], kv[1]


def memory_attention(q, mem_k, mem_v):
    B, T = q.shape[0], q.shape[1]
    s = jnp.einsum('bthd,bmhd->bhtm', q, mem_k).astype(jnp.float32) * X_SCALE
    p = jax.nn.softmax(s, axis=-1).astype(mem_v.dtype)
    return jnp.einsum('bhtm,bmhd->bthd', p, mem_v).reshape(B, T, X_W)


def mixer_output(x, mla_o, rwkv_o, x_o, g_mla, g_rwkv, g_x, w_out):
    mixed = jnp.concatenate([mla_o * jax.nn.silu(g_mla), rwkv_o * jax.nn.silu(g_rwkv), x_o * jax.nn.silu(g_x)], axis=-1)
    return x + jnp.einsum('btc,cd->btd', mixed, w_out)


def setup_inputs(seed: int = 0) -> dict:
    key = jax.random.key(seed)
    ks = iter(jax.random.split(key, 40))
    f32 = jnp.float32
    nrm = lambda shape, scale: jax.random.normal(next(ks), shape, f32) * scale
    n_pages = PAST_LEN // PAGE_SIZE
    n_used = DEC_BATCH * n_pages
    n_pool = n_used + n_used // 4
    d = {}
    d['x_prompt'] = nrm((BATCH, SEQ, D_MODEL), 1.0)
    d['x_sample'] = nrm((DEC_BATCH, DEC_SEQ, D_MODEL), 1.0)
    d['mem_prompt'] = nrm((BATCH, N_MEM, D_MODEL), 1.0)
    d['cache_ckv'] = nrm((DEPTH, n_pool, PAGE_SIZE, KV_LORA), 1.0)
    d['cache_kpe'] = nrm((DEPTH, n_pool, PAGE_SIZE, QK_ROPE), 1.0)
    d['page_table'] = jax.random.permutation(next(ks), n_pool)[:n_used].reshape(DEC_BATCH, n_pages).astype(jnp.int32)
    d['state_wkv'] = nrm((DEPTH, DEC_BATCH, RWKV_HEADS, RWKV_N, RWKV_N), 0.1)
    d['state_shift'] = nrm((DEPTH, DEC_BATCH, SHIFT_COLS), 1.0)
    d['cache_mem_k'] = nrm((DEPTH, DEC_BATCH, N_MEM, X_HEADS, X_HEAD_DIM), 1.0)
    d['cache_mem_v'] = nrm((DEPTH, DEC_BATCH, N_MEM, X_HEADS, X_HEAD_DIM), 1.0)
    d['ln_g'] = 1.0 + nrm((DEPTH, D_MODEL), 0.05)
    d['w_in'] = nrm((DEPTH, D_MODEL, IN_COLS), D_MODEL ** -0.5)
    d['q_norm_g'] = 1.0 + nrm((DEPTH, Q_LORA), 0.05)
    d['kv_norm_g'] = 1.0 + nrm((DEPTH, KV_LORA), 0.05)
    d['w_uq'] = nrm((DEPTH, Q_LORA, MLA_HEADS, QK_NOPE + QK_ROPE), Q_LORA ** -0.5)
    d['w_uk'] = nrm((DEPTH, KV_LORA, MLA_HEADS, QK_NOPE), KV_LORA ** -0.5)
    d['w_uv'] = nrm((DEPTH, KV_LORA, MLA_HEADS, V_HEAD), KV_LORA ** -0.5)
    d['shift_mu'] = jax.random.uniform(next(ks), (DEPTH, SHIFT_COLS), f32, 0.0, 1.0)
    d['w0'] = jax.random.uniform(next(ks), (DEPTH, RWKV_W), f32, -2.0, 1.0)
    d['w_up'] = nrm((DEPTH, DECAY_LORA, RWKV_W), 0.1 * DECAY_LORA ** -0.5)
    d['a0'] = nrm((DEPTH, RWKV_W), 0.1)
    d['a_up'] = nrm((DEPTH, AAA_LORA, RWKV_W), 0.5 * AAA_LORA ** -0.5)
    d['k_k'] = 0.85 + nrm((DEPTH, RWKV_W), 0.05)
    d['k_a'] = 1.0 + nrm((DEPTH, RWKV_W), 0.05)
    d['r_k'] = nrm((DEPTH, RWKV_HEADS, RWKV_N), 0.1)
    d['lnx_g'] = 1.0 + nrm((DEPTH, RWKV_W), 0.05)
    d['lnx_b'] = nrm((DEPTH, RWKV_W), 0.01)
    d['mem_norm_g'] = 1.0 + nrm((DEPTH, D_MODEL), 0.05)
    d['w_mem_kv'] = nrm((DEPTH, D_MODEL, 2, X_HEADS, X_HEAD_DIM), D_MODEL ** -0.5)
    d['w_out'] = nrm((DEPTH, MIX_W, D_MODEL), MIX_W ** -0.5)
    d['final_g'] = 1.0 + nrm((D_MODEL,), 0.05)
    return d


def reference(x_prompt, x_sample, mem_prompt, cache_ckv, cache_kpe, page_table, state_wkv, state_shift,
              cache_mem_k, cache_mem_v, ln_g, w_in, q_norm_g, kv_norm_g, w_uq, w_uk, w_uv, shift_mu, w0, w_up,
              a0, a_up, k_k, k_a, r_k, lnx_g, lnx_b, mem_norm_g, w_mem_kv, w_out, final_g):
    B, S, _ = x_prompt.shape
    DB, T, _ = x_sample.shape
    past_len = page_table.shape[1] * cache_ckv.shape[2]
    pos_p = jnp.arange(S, dtype=jnp.int32)
    pos_s = past_len + jnp.arange(T, dtype=jnp.int32)
    xp, xs = x_prompt, x_sample
    p_ckv, p_kpe, p_wkv, p_shift, p_mk, p_mv = [], [], [], [], [], []
    s_ckv, s_kpe, s_wkv, s_shift = [], [], [], []
    for l in range(DEPTH):
        rw = (shift_mu[l], w0[l], w_up[l], a0[l], a_up[l], k_k[l], k_a[l], r_k[l], lnx_g[l], lnx_b[l])
        qn, qp, ckv, kpe, g_mla, z, g_rwkv, qx, gx = mixer_inputs(xp, pos_p, ln_g[l], w_in[l], q_norm_g[l], kv_norm_g[l], w_uq[l])
        k_nope = jnp.einsum('btc,chd->bthd', ckv, w_uk[l])
        v = jnp.einsum('btc,chd->bthd', ckv, w_uv[l])
        mla_o = mla_prompt_attention(qn, qp, k_nope, kpe, v)
        zero_shift = jnp.zeros((B, SHIFT_COLS), xp.dtype)
        zero_wkv = jnp.zeros((B, RWKV_HEADS, RWKV_N, RWKV_N), xp.dtype)
        rw_o, wkv_p, shift_p = rwkv7_branch(z, zero_shift, zero_wkv, *rw)
        mk, mv = memory_kv(mem_prompt, mem_norm_g[l], w_mem_kv[l])
        x_o = memory_attention(qx, mk, mv)
        xp = mixer_output(xp, mla_o, rw_o, x_o, g_mla, g_rwkv, gx, w_out[l])
        p_ckv.append(ckv); p_kpe.append(kpe); p_wkv.append(wkv_p); p_shift.append(shift_p); p_mk.append(mk); p_mv.append(mv)
        qn, qp, ckv, kpe, g_mla, z, g_rwkv, qx, gx = mixer_inputs(xs, pos_s, ln_g[l], w_in[l], q_norm_g[l], kv_norm_g[l], w_uq[l])
        ckv_past = cache_ckv[l][page_table].reshape(DB, past_len, KV_LORA)
        kpe_past = cache_kpe[l][page_table].reshape(DB, past_len, QK_ROPE)
        mla_o = mla_sample_attention(qn, qp, ckv, kpe, ckv_past, kpe_past, w_uk[l], w_uv[l])
        rw_o, wkv_s, shift_s = rwkv7_branch(z, state_shift[l], state_wkv[l], *rw)
        x_o = memory_attention(qx, cache_mem_k[l], cache_mem_v[l])
        xs = mixer_output(xs, mla_o, rw_o, x_o, g_mla, g_rwkv, gx, w_out[l])
        s_ckv.append(ckv); s_kpe.append(kpe); s_wkv.append(wkv_s); s_shift.append(shift_s)
    y_prompt = rmsnorm(xp, final_g)
    y_sample = rmsnorm(xs, final_g)
    return (y_prompt, y_sample,
            jnp.stack(p_ckv), jnp.stack(p_kpe), jnp.stack(p_wkv), jnp.stack(p_shift), jnp.stack(p_mk), jnp.stack(p_mv),
            jnp.stack(s_ckv), jnp.stack(s_kpe), jnp.stack(s_wkv), jnp.stack(s_shift))
```

```python
import contextlib
import os
import numpy as np
import concourse.bass as bass
import concourse.mybir as mybir
from concourse.bass_utils import run_bass_kernel_spmd

F32 = mybir.dt.float32
BF16 = mybir.dt.bfloat16
I32 = mybir.dt.int32
AF = mybir.ActivationFunctionType
ALU = mybir.AluOpType
AX = mybir.AxisListType

KDEV = os.environ.get("KDEV") == "1"
NPOOL = 64 if KDEV else 10240
KSUB = int(os.environ.get("KSUB", "9"))
KP = int(os.environ.get("KP", "9"))
KQ = int(os.environ.get("KQ", "9"))
KR = int(os.environ.get("KR", "9"))
KT = int(os.environ.get("KT", "3"))
STAGE = int(os.environ.get("KSTAGE", "5"))
NEG = -30000.0
MLA_SCALE = 96 ** -0.5
X_SCALE = 64 ** -0.5


class Buf:
    __slots__ = ("w", "r", "excl")

    def __init__(self):
        self.w = None
        self.r = []
        self.excl = False


class Vw:
    __slots__ = ("ap", "b")

    def __init__(self, ap, b):
        self.ap = ap
        self.b = b


class Tl:
    def __init__(self, t):
        self.t = t
        self.b = Buf()

    def __getitem__(self, idx):
        return Vw(self.t[idx], self.b)


class Sched:
    EPOCH = 3000

    def __init__(self, nc, n_dma_sems=32):
        self.nc = nc
        self.eng = {"pe": nc.tensor, "dve": nc.vector, "act": nc.scalar,
                    "pool": nc.gpsimd, "sp": nc.sync}
        self.sem = {}
        self.cnt = {}
        self.nsem = 0
        self.keep = []
        for e in self.eng:
            self._new_sem(e)
        self.waited = {e: {} for e in self.eng}
        self.pending = {}
        self.dma_sems = []
        for i in range(n_dma_sems):
            self.dma_sems.append([nc.alloc_semaphore(f"dq{i}"), 0])
        self.dma_rr = 0
        self.gq_sems = [[nc.alloc_semaphore(f"gq{i}"), 0] for i in range(40)]
        self.gq_rr = 0
        self.n_inst = 0

    def _new_sem(self, e):
        self.sem[e] = self.nc.alloc_semaphore(f"s_{e}_{self.nsem}")
        self.keep.append(self.sem[e])
        self.nsem += 1
        self.cnt[e] = 0

    def _wait(self, e, deps):
        best = {}
        for (s, v) in deps:
            k = id(s)
            if k not in best or best[k][1] < v:
                best[k] = (s, v)
        for k, (s, v) in best.items():
            if self.waited[e].get(k, 0) >= v:
                continue
            self.eng[e].wait_ge(s, v)
            self.waited[e][k] = v

    def _record(self, ev, reads, writes):
        for b in writes:
            b.w = ev
            b.r = []
        for b in reads:
            if b.w is ev:
                continue
            b.r.append(ev)
            if len(b.r) > 8:
                best = {}
                for (s, v) in b.r:
                    k = id(s)
                    if k not in best or best[k][1] < v:
                        best[k] = (s, v)
                b.r = list(best.values())

    def op(self, e, fn, reads=(), writes=(), pe_acc=False, inc=True):
        reads = [b for b in reads if b is not None]
        writes = [b for b in writes if b is not None]
        own = id(self.sem[e])
        deps = []
        for b in reads:
            if b.w is not None:
                deps.append(b.w)
            if b.excl:
                deps.extend(ev_ for ev_ in b.r if id(ev_[0]) != own)
        for b in writes:
            if b.w is not None and not (pe_acc and id(b.w[0]) == own):
                deps.append(b.w)
            deps.extend(b.r)
        self._wait(e, deps)
        if self.cnt[e] >= self.EPOCH and not self.pending.get(e, False):
            self._new_sem(e)
        ins = fn(self.eng[e])
        self.n_inst += 1
        if inc:
            self.cnt[e] += 1
            ins.then_inc(self.sem[e], 1)
            ev = (self.sem[e], self.cnt[e])
            self.pending[e] = False
        else:
            ev = (self.sem[e], self.cnt[e] + 1)
            self.pending[e] = True
        self._record(ev, reads, writes)
        return ev

    def dma(self, e, out, in_, reads=(), writes=(), **kw):
        reads = [b for b in reads if b is not None]
        writes = [b for b in writes if b is not None]
        deps = []
        for b in reads:
            if b.w is not None:
                deps.append(b.w)
        for b in writes:
            if b.w is not None:
                deps.append(b.w)
            deps.extend(b.r)
        slot = self.gq_sems[self.gq_rr]
        self.gq_rr = (self.gq_rr + 1) % len(self.gq_sems)
        if slot[1] > 0:
            deps.append((slot[0], slot[1]))
        self._wait(e, deps)
        ins = self.eng[e].dma_start(out=out, in_=in_, **kw)
        slot[1] += 16
        ins.then_inc(slot[0], 16)
        ev = (slot[0], slot[1])
        self._record(ev, reads, writes)
        self.n_inst += 1
        return ev

    def dma_gather(self, out, in_, idx, reads=(), writes=()):
        e = "pool"
        reads = [b for b in reads if b is not None]
        writes = [b for b in writes if b is not None]
        deps = []
        for b in reads:
            if b.w is not None:
                deps.append(b.w)
        for b in writes:
            if b.w is not None:
                deps.append(b.w)
            deps.extend(b.r)
        slot = self.gq_sems[self.gq_rr]
        self.gq_rr = (self.gq_rr + 1) % len(self.gq_sems)
        if slot[1] > 0:
            deps.append((slot[0], slot[1]))
        self._wait(e, deps)
        ins = self.nc.gpsimd.indirect_dma_start(out=out, out_offset=None, in_=in_,
                                                in_offset=bass.IndirectOffsetOnAxis(ap=idx, axis=0))
        slot[1] += 16
        ins.then_inc(slot[0], 16)
        ev = (slot[0], slot[1])
        self._record(ev, reads, writes)
        self.n_inst += 1
        return ev

    def wait_bufs(self, e, bufs):
        deps = []
        for b in bufs:
            if b.w is not None:
                deps.append(b.w)
        self._wait(e, deps)

    def barrier(self):
        deps = []
        for s, v in self.dma_sems + self.gq_sems:
            if v > 0:
                deps.append((s, v))
        for e in self.eng:
            if self.cnt[e] > 0:
                deps.append((self.sem[e], self.cnt[e]))
        for e in self.eng:
            self._wait(e, deps)

    def finish(self):
        deps = []
        for s, v in self.dma_sems + self.gq_sems:
            if v > 0:
                deps.append((s, v))
        for e in self.eng:
            if self.cnt[e] > 0:
                deps.append((self.sem[e], self.cnt[e]))
        self._wait("sp", deps)


IN_SPECS = [
    ("x_seq", [4096, 1024], F32), ("x_own", [2048, 1024], F32), ("x_smp", [512, 1024], F32),
    ("mem", [256, 1024], F32),
    ("cs_seq", [4096, 64], F32), ("cs_own", [2048, 64], F32), ("cs_smp", [512, 64], F32), ("masks_s", [128, 256], F32),
    ("masks", [128, 256], F32), ("sel", [128, 2], F32), ("ident", [128, 128], F32),
    ("tri64", [128, 640], F32), ("tri8", [128, 640], F32),
    ("cache_ckv", [NPOOL * 128, 128], F32), ("cache_kpe", [NPOOL * 128, 32], F32), ("iota_p", [128, 1], F32),
    ("page_table", [1, 1024], I32),
    ("state_wkv", [16, 4, 64, 64], F32), ("state_shift", [16, 896], F32),
    ("cache_mem_k", [16, 256, 256], F32), ("cache_mem_v", [16, 256, 256], F32),
    ("ln_g", [128, 8], F32), ("w_in", [1024, 2592], F32), ("q_norm_g", [128, 2], F32),
    ("kv_norm_g", [128], F32), ("w_uq", [256, 768], F32), ("w_uk", [128, 512], F32),
    ("w_uv", [128, 512], F32), ("shift_mu", [896], F32), ("w0", [256], F32), ("w_up", [64, 256], F32),
    ("a0", [256], F32), ("a_up", [64, 256], F32), ("k_k", [256], F32), ("k_a", [256], F32),
    ("r_k", [256], F32), ("lnx_g", [256], F32), ("lnx_b", [256], F32), ("mem_norm_g", [128, 8], F32),
    ("w_mem_kv", [1024, 512], F32), ("w_out", [1024, 1024], F32), ("final_g", [1024], F32),
]
OUT_SPECS = [
    ("y_own", [2048, 1024]), ("y_smp", [128, 1024]), ("ckv_seq", [4096, 128]), ("kpe_seq", [4096, 32]),
    ("wkv_p", [4, 64, 64]), ("shift_p", [1, 896]), ("memk", [256, 256]), ("memv", [256, 256]),
    ("ckv_s", [128, 128]), ("kpe_s", [128, 32]), ("wkv_s", [16, 4, 64, 64]), ("shift_s", [16, 896]),
]


def build_nc():
    nc = bass.Bass("TRN2", target_bir_lowering=False)
    D = {}
    for name, shape, dt in IN_SPECS:
        D[name] = nc.dram_tensor(name, shape, dt, kind="ExternalInput").ap()
    O = {}
    for name, shape in OUT_SPECS:
        O[name] = nc.dram_tensor(name, shape, F32, kind="ExternalOutput").ap()
    S = Sched(nc)
    es = contextlib.ExitStack()
    uid = [0]

    def sb(shape, dt, stack=None):
        uid[0] += 1
        return Tl((stack or es).enter_context(nc.sbuf_tensor(f"sb{uid[0]}", shape, dt)))

    def ps(shape, dt):
        uid[0] += 1
        t = Tl(es.enter_context(nc.psum_tensor(f"ps{uid[0]}", shape, dt)))
        t.b.excl = True
        return t

    def bufs(*vs):
        return [v.b for v in vs if isinstance(v, Vw)]

    def apof(v):
        return v.ap if isinstance(v, Vw) else v

    def dma(e, out, in_):
        o_ap = apof(out)
        i_ap = apof(in_)
        S.dma(e, o_ap, i_ap, reads=bufs(in_), writes=bufs(out))

    def act(out, in_, func, bias=0.0, scale=1.0, accum=None, eng="act"):
        kw = {}
        if accum is not None:
            kw["accum_out"] = accum.ap
        S.op(eng, lambda e: e.activation(out=out.ap, in_=in_.ap, func=func, bias=apof(bias),
                                         scale=apof(scale), **kw),
             reads=bufs(in_, bias, scale), writes=bufs(out, accum))

    def ts(eng, out, in0, s1, s2, op0, op1=None, accum=None):
        kw = {}
        if op1 is not None:
            kw["op1"] = op1
        if accum is not None:
            kw["accum_out"] = accum.ap
        S.op(eng, lambda e: e.tensor_scalar(out=out.ap, in0=in0.ap, scalar1=apof(s1), scalar2=apof(s2),
                                            op0=op0, **kw),
             reads=bufs(in0, s1, s2), writes=bufs(out, accum))

    def tt(eng, out, in0, in1, op):
        S.op(eng, lambda e: e.tensor_tensor(out=out.ap, in0=in0.ap, in1=in1.ap, op=op),
             reads=bufs(in0, in1), writes=bufs(out))

    def stt(eng, out, in0, scalar, in1, op0, op1):
        S.op(eng, lambda e: e.scalar_tensor_tensor(out=out.ap, in0=in0.ap, scalar=apof(scalar), in1=in1.ap,
                                                   op0=op0, op1=op1),
             reads=bufs(in0, scalar, in1), writes=bufs(out))

    def cp(eng, out, in_):
        if eng == "act":
            S.op(eng, lambda e: e.copy(out=out.ap, in_=in_.ap), reads=bufs(in_), writes=bufs(out))
        else:
            S.op(eng, lambda e: e.tensor_copy(out=out.ap, in_=in_.ap), reads=bufs(in_), writes=bufs(out))

    def memset(eng, out, val):
        S.op(eng, lambda e: e.memset(out.ap, val), writes=bufs(out))

    def recip(out, in_):
        S.op("dve", lambda e: e.reciprocal(out=out.ap, in_=in_.ap), reads=bufs(in_), writes=bufs(out))

    def mm(out, lhsT, rhs, start=True, stop=True, last=True):
        bp = lhsT.ap.base_partition()
        kw = {"tile_position": (bp, 0)} if bp else {}
        S.op("pe", lambda e: e.matmul(out.ap, lhsT=lhsT.ap, rhs=rhs.ap, start=start, stop=stop, **kw),
             reads=bufs(lhsT, rhs), writes=bufs(out), pe_acc=True, inc=last)

    def tr(out, in_, ident, last=True):
        S.op("pe", lambda e: e.transpose(out=out.ap, in_=in_.ap, identity=ident.ap),
             reads=bufs(in_, ident), writes=bufs(out), pe_acc=True, inc=last)

    rr = {"ev": 0}

    def evac_eng():
        rr["ev"] ^= 1
        return "act" if rr["ev"] else "dve"

    with es:
        ident_f = sb([128, 128], F32)
        ident_b = sb([128, 128], BF16)
        masks_b = sb([128, 256], BF16)
        for mi_ in range(2):
            dma("sp", ident_f[:], D["masks"][:, mi_ * 128:(mi_ + 1) * 128])
            cp("dve", masks_b[:, mi_ * 128:(mi_ + 1) * 128], ident_f[:])
        dma("sp", ident_f[:], D["ident"])
        cp("dve", ident_b[:], ident_f[:])
        sel_t = sb([128, 2], F32)
        dma("sp", sel_t[:], D["sel"])
        tri64 = sb([128, 640], F32)
        dma("sp", tri64[:], D["tri64"])
        ones_col = sb([128, 1], F32)
        memset("dve", ones_col[:], 1.0)

        def bcast_row(name, n, stack=None):
            t = sb([128, n], F32, stack)
            dma("sp", t[:], D[name].partition_broadcast(128))
            return t

        kvg_bc = bcast_row("kv_norm_g", 128)
        qg_col = sb([128, 2], F32)
        dma("sp", qg_col[:], D["q_norm_g"])
        lng_col = sb([128, 8], F32)
        dma("sp", lng_col[:], D["ln_g"])
        memg_col = sb([128, 8], F32)
        dma("sp", memg_col[:], D["mem_norm_g"])

        W_uk = sb([128, 512], BF16)
        W_uv = sb([128, 512], BF16)
        mv_aug = sb([128, 2, 4, 65], BF16)
        mkT = sb([64, 4, 256], BF16)
        WUA = sb([64, 2, 256], BF16)
        rw_own = sb([128, 16, 256], BF16)
        rw_smp = sb([128, 4, 256], BF16)
        ckvS_b = sb([128, 4, 128], BF16)
        ckvTS = sb([128, 4, 128], BF16)
        kpeTS = sb([32, 4, 128], BF16)
        Hf = [sb([128, 128], F32) for _ in range(2)]
        Hb = [sb([128, 128], BF16) for _ in range(2)]
        for t_ in Hf:
            memset("pool", t_[:], 0.0)
        for t_ in Hb:
            memset("pool", t_[:], 0.0)
        kT = sb([96, 8, 4096], BF16)
        Vaug = sb([128, 32, 8, 65], BF16)
        memset("pool", Vaug[:], 1.0)

        ps_mm = [ps([128, 512], F32) for _ in range(4)]
        ps_tr = [ps([128, 1024], BF16) for _ in range(2)]
        ps_acc = [ps([128, 512], F32) for _ in range(2)]
        cnt = {"mm": 0, "tr": 0}

        def next_mm():
            cnt["mm"] += 1
            return ps_mm[cnt["mm"] % 4]

        def next_tr():
            cnt["tr"] += 1
            return ps_tr[cnt["tr"] % 2]

        phA = contextlib.ExitStack()
        mu_bc = bcast_row("shift_mu", 896, phA)
        w0_bc = bcast_row("w0", 256, phA)
        a0_bc = bcast_row("a0", 256, phA)
        kkp_bc = bcast_row("k_k", 256, phA)
        ka_bc = bcast_row("k_a", 256, phA)
        rk_bc = bcast_row("r_k", 256, phA)
        lg_bc = bcast_row("lnx_g", 256, phA)
        lb_bc = bcast_row("lnx_b", 256, phA)
        W_A = sb([128, 8, 1056], BF16, phA)
        xt = [sb([128, 1024], F32, phA)] * 2
        xn = [sb([128, 1024], BF16, phA)] * 2
        hT = [sb([128, 8, 128], BF16, phA)] * 2
        ss = [sb([128, 4], F32, phA) for _ in range(2)]
        cst = [sb([128, 64], F32, phA)] * 2
        ckv_f = [sb([128, 128], F32, phA)] * 2
        ckv_b = [sb([128, 128], BF16, phA)] * 2
        ckvT = [sb([128, 128], BF16, phA)] * 2
        kpe_f = [sb([128, 32], F32, phA)] * 2
        kpad = [sb([128, 96], BF16, phA)] * 2
        for t in kpad:
            memset("pool", t[:], 0.0)
        rtmp = [sb([128, 64], F32, phA)] * 2
        zt = [sb([128, 896], F32, phA)] * 2

        def load_norm_T(i, src_rows, g_cols=1024.0):
            s = i % 2
            dma("sp", xt[s][:], src_rows)
            act(xn[s][:], xt[s][:], AF.Square, scale=1.0 / 32.0, accum=ss[s][:, 0:1])
            act(ss[s][:, 1:2], ss[s][:, 0:1], AF.Sqrt, bias=1e-6)
            recip(ss[s][:, 2:3], ss[s][:, 1:2])
            ts("dve", xn[s][:], xt[s][:], ss[s][:, 2:3], None, ALU.mult)
            p = next_tr()
            for k in range(8):
                tr(p[:, k * 128:(k + 1) * 128], xn[s][:, k * 128:(k + 1) * 128], ident_b[:], last=(k == 7))
            cp("act", hT[s][:], Vw(p.t[:].rearrange("p (k t) -> p k t", k=8), p.b))
            return s

        def proj(s, W, c0, c1):
            p = next_mm()
            n = c1 - c0
            for k in range(8):
                mm(p[:, 0:n], hT[s][:, k, :], W[:, k, c0:c1], start=(k == 0), stop=(k == 7), last=(k == 7))
            return p

        def rope_pairs(out_f, src, tab, nh):
            pass

        wtmp = contextlib.ExitStack()
        stage_w = [sb([128, 1056], F32, wtmp), sb([128, 1056], F32, wtmp)]
        W_mem = sb([128, 8, 512], BF16, wtmp)
        mk_b = sb([128, 2, 256], BF16, wtmp)
        mkv_f = [sb([128, 512], F32, wtmp) for _ in range(2)]
        w_in_v = D["w_in"].rearrange("(k p) n -> k p n", p=128)
        for k in range(8):
            st = stage_w[k % 2]
            dma("sp", st[:, 0:160], w_in_v[k][:, 256:416])
            dma("sp", st[:, 160:1056], w_in_v[k][:, 928:1824])
            ts("dve" if k % 2 else "pool", W_A[:, k, :], st[:], lng_col[:, k:k + 1], None, ALU.mult)
        w_mem_v = D["w_mem_kv"].rearrange("(k p) n -> k p n", p=128)
        for k in range(8):
            st = stage_w[k % 2]
            dma("sp", st[:, 0:512], w_mem_v[k])
            ts("dve" if k % 2 else "pool", W_mem[:, k, :], st[:, 0:512], memg_col[:, k:k + 1], None, ALU.mult)
        st = stage_w[0]
        dma("sp", st[:, 0:512], D["w_uk"])
        cp("dve", W_uk[:], st[:, 0:512])
        st = stage_w[1]
        dma("sp", st[:, 0:512], D["w_uv"])
        cp("dve", W_uv[:], st[:, 0:512])
        st = stage_w[0]
        dma("sp", st[0:64, 0:256], D["w_up"])
        dma("sp", st[0:64, 256:512], D["a_up"])
        cp("dve", Vw(WUA.t[:].rearrange("p a n -> p (a n)"), WUA.b), st[0:64, 0:512])

        memset("pool", mv_aug[:], 1.0)
        for i in range(2):
            s = load_norm_T(i, D["mem"][i * 128:(i + 1) * 128, :])
            p = proj(s, W_mem, 0, 512)
            cp("act", mkv_f[i][:], p[:, :])
            dma("sp", O["memk"][i * 128:(i + 1) * 128, :], mkv_f[i][:, 0:256])
            dma("sp", O["memv"][i * 128:(i + 1) * 128, :], mkv_f[i][:, 256:512])
            cp("dve", mk_b[:, i, :], mkv_f[i][:, 0:256])
            cp("dve", Vw(mv_aug.t[:, i, :, 0:64], mv_aug.b),
               Vw(mkv_f[i].t[:, 256:512].rearrange("p (h d) -> p h d", h=4), mkv_f[i].b))
            pt_ = next_tr()
            for h in range(4):
                tr(pt_[0:64, h * 128:(h + 1) * 128], mk_b[:, i, h * 64:(h + 1) * 64], ident_b[:], last=(h == 3))
            cp("act", Vw(mkT.t[:, :, i * 128:(i + 1) * 128], mkT.b),
               Vw(pt_.t[0:64, 0:512].rearrange("p (h t) -> p h t", h=4), pt_.b))

        S.barrier()
        wtmp.close()

        phR = contextlib.ExitStack()
        zprev = sb([128, 896], F32, phR)
        zm = zprev
        ta = sb([128, 128], BF16, phR)
        taT = sb([64, 2, 128], BF16, phR)
        f_ = lambda: sb([128, 256], F32, phR)
        b_ = lambda: sb([128, 256], BF16, phR)
        xw, lw, a_t, kk, kp, beta, tmp1, Gs, Ab_f = [f_() for _ in range(9)]
        e2, tmp2, e1, Y_f, rw_o = xw, Gs, a_t, kk, beta
        Ab, Rb, Bt, Kt, Bh, Kh, v_b = [b_() for _ in range(7)]
        U_c = [b_() for _ in range(4)]
        v_c = [b_() for _ in range(4)]
        for t_ in U_c:
            memset("pool", t_[:], 0.0)
        red = sb([128, 16], F32, phR)
        T4h = sb([64, 4, 4, 128], BF16, phR)
        RT = [sb([128, 128], BF16, phR) for _ in range(2)]
        Lm = [[sb([128, 128], BF16, phR) for _ in range(2)] for _ in range(4)]
        LT = [[sb([128, 128], BF16, phR) for _ in range(2)] for _ in range(4)]
        LA = [sb([128, 256], BF16, phR) for _ in range(4)]
        KA = [sb([128, 256], BF16, phR) for _ in range(4)]
        X_f = [sb([128, 128], F32, phR) for _ in range(4)]
        X_b = [sb([128, 128], BF16, phR) for _ in range(4)]
        WT = [sb([128, 128], BF16, phR) for _ in range(2)]
        gam = [sb([128, 2], F32, phR) for _ in range(2)]

        def rwkv_prep(z, tri, nlev):
            tt("dve", zm[:], zprev[:], z[:], ALU.subtract)
            tt("pool", zm[:], zm[:], mu_bc[:], ALU.mult)
            tt("dve", zm[:], zm[:], z[:], ALU.add)
            r_ = zm[:, 0:256]
            k_ = zm[:, 256:512]
            v_ = zm[:, 512:768]
            if KP < 2:
                return
            act(ta[:, 0:64], zm[:, 768:832], AF.Tanh)
            cp("pool", ta[:, 64:128], zm[:, 832:896])
            pt_ = next_tr()
            tr(pt_[0:64, 0:128], ta[:, 0:64], ident_b[:], last=False)
            tr(pt_[0:64, 128:256], ta[:, 64:128], ident_b[:])
            cp("act", taT[:], Vw(pt_.t[0:64, 0:256].rearrange("p (a t) -> p a t", a=2), pt_.b))
            if KP < 3:
                return
            p1 = next_mm()
            mm(p1[:, 0:256], taT[:, 0, :], WUA[:, 0, :])
            mm(p1[:, 256:512], taT[:, 1, :], WUA[:, 1, :])
            tt("dve", xw[:], p1[:, 0:256], w0_bc[:], ALU.add)
            tt("dve", a_t[:], p1[:, 256:512], a0_bc[:], ALU.add)
            if KP < 4:
                return
            act(xw[:], xw[:], AF.Exp, scale=-1.0)
            act(xw[:], xw[:], AF.Ln, bias=1.0)
            act(xw[:], xw[:], AF.Exp, scale=-1.0, bias=-0.5)
            ts("pool", lw[:], xw[:], -1.0, None, ALU.mult)
            act(a_t[:], a_t[:], AF.Sigmoid)
            if KSUB < 2:
                return
            tt("dve", kk[:], k_, kkp_bc[:], ALU.mult)
            tt("pool", tmp1[:], kk[:], kk[:], ALU.mult)
            S.op("dve", lambda e: e.reduce_sum(out=red.t[:, 0:4], in_=tmp1.t[:].rearrange("p (h k) -> p h k", h=4), axis=AX.X),
                 reads=[tmp1.b], writes=[red.b])
            act(red[:, 0:4], red[:, 0:4], AF.Sqrt)
            ts("dve", red[:, 0:4], red[:, 0:4], 1e-12, None, ALU.max)
            recip(red[:, 0:4], red[:, 0:4])
            for h in range(4):
                ts("dve" if h % 2 else "pool", kk[:, h * 64:(h + 1) * 64], kk[:, h * 64:(h + 1) * 64], red[:, h:h + 1], None, ALU.mult)
            stt("dve", kp[:], a_t[:], -1.0, ka_bc[:], ALU.add, ALU.mult)
            stt("dve", kp[:], kp[:], 1.0, k_, ALU.add, ALU.mult)
            tt("pool", beta[:], kk[:], a_t[:], ALU.mult)
            tt("pool", tmp1[:], r_, kp[:], ALU.mult)
            tt("dve", tmp1[:], tmp1[:], rk_bc[:], ALU.mult)
            S.op("dve", lambda e: e.reduce_sum(out=red.t[:, 4:8], in_=tmp1.t[:].rearrange("p (h k) -> p h k", h=4), axis=AX.X),
                 reads=[tmp1.b], writes=[red.b])
            cp("act", v_b[:], v_)
            if KSUB < 3:
                return
            pg = next_mm()
            mm(pg[:, 0:256], tri[:, 0:128], lw[:])
            mm(pg[:, 256:512], tri[:, 128:256], lw[:])
            cp("act", Gs[:], pg[:, 0:256])
            if KQ < 2:
                return
            tt("dve", e1[:], Gs[:], lw[:], ALU.subtract)
            act(e1[:], e1[:], AF.Exp)
            stt("dve", Ab_f[:], kk[:], -1.0, e1[:], ALU.mult, ALU.mult)
            cp("pool", Ab[:], Ab_f[:])
            if KQ < 3:
                return
            act(e2[:], Gs[:], AF.Exp)
            tt("dve", Rb[:], r_, e2[:], ALU.mult)
            act(e2[:], Gs[:], AF.Exp, scale=-1.0)
            tt("dve", Bt[:], beta[:], e2[:], ALU.mult)
            tt("pool", Kt[:], kp[:], e2[:], ALU.mult)
            if KQ < 4:
                return
            tt("dve", e1[:], pg[:, 256:512], Gs[:], ALU.subtract)
            act(e1[:], e1[:], AF.Exp)
            tt("dve", Bh[:], beta[:], e1[:], ALU.mult)
            tt("pool", Kh[:], kp[:], e1[:], ALU.mult)
            if KSUB < 4:
                return
            if KR < 1:
                return
            for h in range(4 if (KT & 1) else 0):
                pt_ = next_tr()
                for qi, src_ in enumerate((Ab, Rb, Bt, Kt)):
                    tr(pt_[0:64, qi * 128:(qi + 1) * 128], src_[:, h * 64:(h + 1) * 64], ident_b[:], last=(qi == 3))
                cp(evac_eng(), T4h[:, h, :, :], Vw(pt_.t[0:64, 0:512].rearrange("p (q t) -> p q t", q=4), pt_.b))
            pt_ = next_tr()
            for hp in range(2 if (KT & 2) else 0):
                tr(pt_[:, hp * 128:(hp + 1) * 128], Rb[:, hp * 128:(hp + 1) * 128], ident_b[:], last=(hp == 1))
            for hp in range(2 if (KT & 2) else 0):
                cp(evac_eng(), RT[hp][:], pt_[:, hp * 128:(hp + 1) * 128])
            if KR < 2:
                return
            for h in range(4):
                AbT = T4h[:, h, 0, :]
                ARt = Vw(T4h.t[:, h, 0:2, :].rearrange("p q t -> p (q t)"), T4h.b)
                BtT = T4h[:, h, 2, :]
                KtT = T4h[:, h, 3, :]
                p = next_mm()
                mm(p[:, 0:128], AbT, BtT)
                tt("dve", Lm[h][0][:], p[:, 0:128], tri[:, 512:640], ALU.mult)
                p = next_mm()
                mm(p[:, 0:256], BtT, ARt)
                tt("dve", LA[h][:], p[:, 0:256], tri[:, 256:512], ALU.mult)
                p = next_mm()
                mm(p[:, 0:256], KtT, ARt)
                tt("dve", KA[h][:], p[:, 0:256], tri[:, 256:512], ALU.mult)
                if KR < 3:
                    continue
                p = next_mm()
                mm(p[:, 0:64], KA[h][:, 0:128], v_b[:, h * 64:(h + 1) * 64])
                cp("act", X_f[h][:, 64:128], p[:, 0:64])
                cp("pool", X_f[h][:, 0:64], Ab_f[:, h * 64:(h + 1) * 64])
                cp("pool", X_b[h][:], X_f[h][:])
            for lev in range(nlev if KR >= 4 else 0):
                cur = lev % 2
                for h in range(4):
                    L_cur = Lm[h][cur][:]
                    LT_cur = LA[h][:, 0:128] if lev == 0 else LT[h][cur][:]
                    p = next_mm()
                    mm(p[:, 0:128], LT_cur, X_b[h][:])
                    tt("dve", X_f[h][:], X_f[h][:], p[:, 0:128], ALU.add)
                    cp("pool", X_b[h][:], X_f[h][:])
                    if lev < nlev - 1:
                        p2 = next_mm()
                        mm(p2[:, 0:128], LT_cur, L_cur)
                        mm(p2[:, 128:256], L_cur, LT_cur)
                        cp("act", Lm[h][1 - cur][:], p2[:, 0:128])
                        cp("dve", LT[h][1 - cur][:], p2[:, 128:256])
            for hp in range(2 if KR >= 5 else 0):
                pt_ = next_tr()
                for hh in range(2):
                    h = hp * 2 + hh
                    cp("pool", ta[:, hh * 64:(hh + 1) * 64], X_f[h][:, 0:64])
                tr(pt_[:, 0:128], ta[:], ident_b[:])
                cp("act", WT[hp][:], pt_[:, 0:128])

        def rwkv_mask_v(slot, base, tri):
            ts("pool", v_c[slot][:], v_b[:], tri[:, 128 + base:129 + base], None, ALU.mult)

        def rwkv_seq(base, n, slot, hp, Hf_, Hb_, tri):
            rs = slice(base, base + n)
            cs = slice(hp * 128, (hp + 1) * 128)
            Uc, vc = U_c[slot], v_c[slot]
            ind = tri[:, 128 + base:129 + base]
            pgm = next_mm()
            mm(pgm[:, 0:1], Vw(lw.t[:, cs], lw.b), ind)
            act(gam[hp][:, 0:1], pgm[:, 0:1], AF.Exp)
            pu = next_mm()
            mm(pu[:, 0:128], WT[hp][:], Hb_[:])
            for hh in range(2):
                h = hp * 2 + hh
                tt("dve", Vw(Uc.t[rs, h * 64:(h + 1) * 64], Uc.b), Vw(pu.t[rs, hh * 64:(hh + 1) * 64], pu.b),
                   Vw(X_f[h].t[rs, 64:128], X_f[h].b), ALU.add)
            py = next_mm()
            mm(py[:, 0:128], RT[hp][:], Hb_[:], start=True, stop=False, last=False)
            for hh in range(2):
                h = hp * 2 + hh
                mm(py[:, hh * 64:(hh + 1) * 64], LA[h][:, 128:256], Uc[:, h * 64:(h + 1) * 64],
                   start=False, stop=False, last=False)
                mm(py[:, hh * 64:(hh + 1) * 64], KA[h][:, 128:256], vc[:, h * 64:(h + 1) * 64],
                   start=False, stop=(hh == 1), last=(hh == 1))
            cp("act", Vw(Y_f.t[rs, cs], Y_f.b), Vw(py.t[rs, 0:128], py.b))
            ph_ = next_mm()
            mm(ph_[:, 0:128], Bh[:, cs], Uc[:, cs], start=True, stop=False, last=False)
            mm(ph_[:, 0:128], Kh[:, cs], vc[:, cs], start=False, stop=True, last=True)
            for hh in range(2):
                blk = slice(hh * 64, (hh + 1) * 64)
                stt("dve", Vw(Hf_.t[blk, blk], Hf_.b), Vw(Hf_.t[blk, blk], Hf_.b), Vw(gam[hp].t[blk, 0:1], gam[hp].b),
                    Vw(ph_.t[blk, blk], ph_.b), ALU.mult, ALU.add)
            cp("act", Hb_[:], Hf_[:])

        def rwkv_post(z):
            v_ = zm[:, 512:768]
            S.op("dve", lambda e: e.reduce_sum(out=red.t[:, 8:12], in_=Y_f.t[:].rearrange("p (h k) -> p h k", h=4), axis=AX.X),
                 reads=[Y_f.b], writes=[red.b])
            ts("dve", red[:, 8:12], red[:, 8:12], -1.0 / 64, None, ALU.mult)
            for h in range(4):
                ts("dve" if h % 2 else "pool", tmp2[:, h * 64:(h + 1) * 64], Y_f[:, h * 64:(h + 1) * 64], red[:, 8 + h:9 + h], None, ALU.add)
            tt("pool", tmp1[:], tmp2[:], tmp2[:], ALU.mult)
            S.op("dve", lambda e: e.reduce_sum(out=red.t[:, 12:16], in_=tmp1.t[:].rearrange("p (h k) -> p h k", h=4), axis=AX.X),
                 reads=[tmp1.b], writes=[red.b])
            act(red[:, 12:16], red[:, 12:16], AF.Sqrt, scale=1.0 / 64, bias=64e-5)
            recip(red[:, 12:16], red[:, 12:16])
            for h in range(4):
                ts("dve" if h % 2 else "pool", tmp2[:, h * 64:(h + 1) * 64], tmp2[:, h * 64:(h + 1) * 64], red[:, 12 + h:13 + h], None, ALU.mult)
            tt("dve", tmp2[:], tmp2[:], lg_bc[:], ALU.mult)
            tt("pool", tmp2[:], tmp2[:], lb_bc[:], ALU.add)
            for h in range(4):
                stt("dve", rw_o[:, h * 64:(h + 1) * 64], Vw(zm.t[:, 512 + h * 64:512 + (h + 1) * 64], zm.b), red[:, 4 + h:5 + h],
                    tmp2[:, h * 64:(h + 1) * 64], ALU.mult, ALU.add)

        NBLK = int(os.environ.get("KNBLK", "32")) if STAGE >= 1 else 0
        for i in range(NBLK):
            s = load_norm_T(i, D["x_seq"][i * 128:(i + 1) * 128, :])
            dma("sp", cst[s][:], D["cs_seq"][i * 128:(i + 1) * 128, :])
            p = proj(s, W_A, 0, 160)
            act(ckv_b[s][:], p[:, 0:128], AF.Square, scale=128 ** -0.5, accum=ss[s][:, 3:4])
            act(ss[s][:, 3:4], ss[s][:, 3:4], AF.Sqrt, bias=1e-6)
            recip(ss[s][:, 3:4], ss[s][:, 3:4])
            stt("dve", ckv_f[s][:], p[:, 0:128], ss[s][:, 3:4], kvg_bc[:], ALU.mult, ALU.mult)
            dma("sp", O["ckv_seq"][i * 128:(i + 1) * 128, :], ckv_f[s][:])
            cp("pool", ckv_b[s][:], ckv_f[s][:])
            tt("dve", rtmp[s][:, 0:32], p[:, 128:160], cst[s][:, 0:32], ALU.mult)
            tt("dve", rtmp[s][:, 32:64], p[:, 128:160], cst[s][:, 32:64], ALU.mult)
            tt("dve", kpe_f[s][:, 0:16], rtmp[s][:, 0:16], rtmp[s][:, 48:64], ALU.subtract)
            tt("dve", kpe_f[s][:, 16:32], rtmp[s][:, 16:32], rtmp[s][:, 32:48], ALU.add)
            dma("sp", O["kpe_seq"][i * 128:(i + 1) * 128, :], kpe_f[s][:])
            cp("pool", kpad[s][:, 64:96], kpe_f[s][:])
            pt_ = next_tr()
            tr(pt_[:, 0:128], ckv_b[s][:], ident_b[:], last=False)
            tr(pt_[0:96, 128:256], kpad[s][:], ident_b[:], last=True)
            cp("act", ckvT[s][:], pt_[:, 0:128])
            for h in range(8):
                cp("dve" if h % 2 else "act", kT[64:96, h, i * 128:(i + 1) * 128], pt_[64:96, 128:256])
            p = next_mm()
            mm(p[:, :], ckvT[s][:], W_uv[:], last=True)
            cp("dve", Vw(Vaug.t[:, i, :, 0:64], Vaug.b), Vw(p.t[:, :].rearrange("p (h d) -> p h d", h=8), p.b))
            for hh in range(2):
                p = next_mm()
                for h4 in range(4):
                    h = hh * 4 + h4
                    mm(p[0:64, h4 * 128:(h4 + 1) * 128], W_uk[:, h * 64:(h + 1) * 64], ckvT[s][:], last=(h4 == 3))
                cp("act", Vw(kT.t[0:64, hh * 4:(hh + 1) * 4, i * 128:(i + 1) * 128], kT.b),
                   Vw(p.t[0:64, :].rearrange("p (h t) -> p h t", h=4), p.b))
            if STAGE >= 2:
                if i == 0:
                    memset("pool", zprev[0:1, :], 0.0)
                else:
                    dma("sp", zprev[0:1, :], zt[s][127:128, :])
            for half in range(2):
                p = proj(s, W_A, 160 + half * 448, 160 + (half + 1) * 448)
                cp(evac_eng(), zt[s][:, half * 448:(half + 1) * 448], p[:, 0:448])
            if i == NBLK - 1:
                dma("sp", O["shift_p"], zt[s][127:128, :])
            if STAGE >= 2:
                dma("sp", zprev[1:128, :], zt[s][0:127, :])
                if KSUB >= 1:
                    rwkv_prep(zt[s], tri64, 6)
                for c in range(2):
                    if KSUB >= 5:
                        rwkv_mask_v(c, c * 64, tri64)
                    for hp in range(2):
                        if KSUB >= 5:
                            rwkv_seq(c * 64, 64, c, hp, Hf[hp], Hb[hp], tri64)
                if KSUB >= 6:
                    rwkv_post(zt[s])
                if i % 2 == 0:
                    ts("dve", rw_own[:, i // 2, :], rw_o[:], sel_t[:, 0:1], None, ALU.mult)
                else:
                    stt("dve", rw_own[:, i // 2, :], rw_o[:], sel_t[:, 1:2], rw_own[:, i // 2, :], ALU.mult, ALU.add)
        stS = sb([128, 128], F32, phR)
        if STAGE >= 2 and os.environ.get("KFIN", "1") == "1":
            for hp in range(2):
                ptf = next_mm()
                tr(ptf[:, 0:128], Hf[hp][:], ident_f[:])
                cp("act", stS[:], ptf[:, 0:128])
                for hh in range(2):
                    blk = slice(hh * 64, (hh + 1) * 64)
                    dma("sp", O["wkv_p"][hp * 2 + hh], Vw(stS.t[blk, blk], stS.b))

        if STAGE >= 4:
            tri8 = sb([128, 640], F32, phR)
            dma("sp", tri8[:], D["tri8"])
            Hs_f = sb([128, 128], F32, phR)
            Hs_b = sb([128, 128], BF16, phR)
            memset("pool", Hs_f[:], 0.0)
            Sin = sb([64, 128], F32, phR)
            stS2 = stS
            kpe_b = sb([128, 32], BF16, phR)
            for t_ in U_c:
                memset("pool", t_[:], 0.0)
            for tau in range(4):
                i = 32 + tau
                s = load_norm_T(i, D["x_smp"][tau * 128:(tau + 1) * 128, :])
                dma("sp", cst[s][:], D["cs_smp"][tau * 128:(tau + 1) * 128, :])
                p = proj(s, W_A, 0, 160)
                act(ckv_b[s][:], p[:, 0:128], AF.Square, scale=128 ** -0.5, accum=ss[s][:, 3:4])
                act(ss[s][:, 3:4], ss[s][:, 3:4], AF.Sqrt, bias=1e-6)
                recip(ss[s][:, 3:4], ss[s][:, 3:4])
                stt("dve", ckv_f[s][:], p[:, 0:128], ss[s][:, 3:4], kvg_bc[:], ALU.mult, ALU.mult)
                cp("pool", ckvS_b[:, tau, :], ckv_f[s][:])
                tt("dve", rtmp[s][:, 0:32], p[:, 128:160], cst[s][:, 0:32], ALU.mult)
                tt("dve", rtmp[s][:, 32:64], p[:, 128:160], cst[s][:, 32:64], ALU.mult)
                tt("dve", kpe_f[s][:, 0:16], rtmp[s][:, 0:16], rtmp[s][:, 48:64], ALU.subtract)
                tt("dve", kpe_f[s][:, 16:32], rtmp[s][:, 16:32], rtmp[s][:, 32:48], ALU.add)
                cp("pool", kpe_b[:], kpe_f[s][:])
                for q in range(4):
                    sq = 4 * tau + q
                    dma("sp", O["ckv_s"][sq * 8:(sq + 1) * 8, :], ckv_f[s][32 * q:32 * q + 8, :])
                    dma("sp", O["kpe_s"][sq * 8:(sq + 1) * 8, :], kpe_f[s][32 * q:32 * q + 8, :])
                pt_ = next_tr()
                tr(pt_[:, 0:128], ckvS_b[:, tau, :], ident_b[:], last=False)
                tr(pt_[0:32, 128:256], kpe_b[:], ident_b[:], last=True)
                cp("act", ckvTS[:, tau, :], pt_[:, 0:128])
                cp("dve", kpeTS[:, tau, :], pt_[0:32, 128:256])
                for half in range(2):
                    p = proj(s, W_A, 160 + half * 448, 160 + (half + 1) * 448)
                    cp(evac_eng(), zt[s][:, half * 448:(half + 1) * 448], p[:, 0:448])
                dma("sp", zprev[1:128, :], zt[s][0:127, :])
                for q in range(4):
                    sq = 4 * tau + q
                    dma("sp", O["shift_s"][sq:sq + 1, :], zt[s][32 * q + 7:32 * q + 8, :])
                    dma("sp", zprev[32 * q:32 * q + 1, :], D["state_shift"][sq:sq + 1, :])
                rwkv_prep(zt[s], tri8, 3)
                for q in range(4):
                    sq = 4 * tau + q
                    rwkv_mask_v(q, 32 * q, tri8)
                    for hp in range(2):
                        dma("sp", Vw(Sin.t[:].rearrange("v (h k) -> v h k", h=2), Sin.b),
                            D["state_wkv"][sq, 2 * hp:2 * hp + 2].rearrange("h v k -> v h k"))
                        ptf = next_mm()
                        tr(ptf[:, 0:64], Sin[:], ident_f[0:64, 0:64])
                        cp("act", Hs_f[0:64, 0:64], ptf[0:64, 0:64])
                        cp("dve", Hs_f[64:128, 64:128], ptf[64:128, 0:64])
                        cp("act", Hs_b[:], Hs_f[:])
                        rwkv_seq(32 * q, 8, q, hp, Hs_f, Hs_b, tri8)
                        ptf = next_mm()
                        tr(ptf[:, 0:128], Hs_f[:], ident_f[:])
                        cp("act", stS2[:], ptf[:, 0:128])
                        for hh in range(2):
                            blk = slice(hh * 64, (hh + 1) * 64)
                            dma("sp", O["wkv_s"][sq, hp * 2 + hh], Vw(stS2.t[blk, blk], stS2.b))
                rwkv_post(zt[s])
                cp("pool", rw_smp[:, tau, :], rw_o[:])

        S.barrier()
        phR.close()
        phA.close()

        if STAGE >= 3:
            phB = contextlib.ExitStack()
            GQ = 2
            NG = 16 // GQ
            NQ = GQ * 128
            W_B = sb([128, 8, 1536], BF16, phB)
            W_uq = sb([128, 2, 768], BF16, phB)
            W_out = sb([128, 8, 1024], BF16, phB)
            fing_bc = bcast_row("final_g", 1024, phB)
            tmpB = contextlib.ExitStack()
            stB = [sb([128, 1536], F32, tmpB) for _ in range(2)]
            for k in range(8):
                st = stB[k % 2]
                dma("sp", st[:, 0:256], w_in_v[k][:, 0:256])
                dma("sp", st[:, 256:768], w_in_v[k][:, 416:928])
                dma("sp", st[:, 768:1536], w_in_v[k][:, 1824:2592])
                ts("dve" if k % 2 else "pool", W_B[:, k, :], st[:], lng_col[:, k:k + 1], None, ALU.mult)
            w_uq_v = D["w_uq"].rearrange("(k p) n -> k p n", p=128)
            for k in range(2):
                st = stB[k % 2]
                dma("sp", st[:, 0:768], w_uq_v[k])
                ts("dve", W_uq[:, k, :], st[:, 0:768], qg_col[:, k:k + 1], None, ALU.mult)
            w_out_v = D["w_out"].rearrange("(k p) n -> k p n", p=128)
            for k in range(8):
                st = stB[k % 2]
                dma("sp", st[:, 0:1024], w_out_v[k])
                cp("dve" if k % 2 else "pool", W_out[:, k, :], st[:, 0:1024])
            S.barrier()
            tmpB.close()
            phB2 = contextlib.ExitStack()
            xtB = sb([128, 1024], F32, phB2)
            xnB = sb([128, 1024], BF16, phB2)
            hTB = sb([128, 8, 128], BF16, phB2)
            ssB = sb([128, 8], F32, phB2)
            cstB = sb([128, 64], F32, phB2)
            cqn = sb([128, 256], BF16, phB2)
            cqT = sb([128, 2, 128], BF16, phB2)
            q_f = sb([128, 8, 96], F32, phB2)
            q_b = sb([128, 8, 96], BF16, phB2)
            rq = sb([128, 8, 64], F32, phB2)
            qT = sb([96, 8, NQ], BF16, phB2)
            qx_b = sb([128, 256], BF16, phB2)
            qxT = sb([64, 4, NQ], BF16, phB2)
            gate = sb([128, GQ, 1024], BF16, phB2)
            mixed = sb([128, GQ, 1024], BF16, phB2)
            Pt = [sb([128, NQ], BF16, phB2) for _ in range(2)]
            OT = sb([65, NQ], F32, phB2)
            rden = sb([128, GQ], F32, phB2)
            mT = sb([128, 8, 128], BF16, phB2)

            def loadB(src_rows):
                dma("sp", xtB[:], src_rows)
                act(xnB[:], xtB[:], AF.Square, scale=1.0 / 32.0, accum=ssB[:, 0:1])
                act(ssB[:, 1:2], ssB[:, 0:1], AF.Sqrt, bias=1e-6)
                recip(ssB[:, 2:3], ssB[:, 1:2])
                ts("dve", xnB[:], xtB[:], ssB[:, 2:3], None, ALU.mult)
                p = next_tr()
                for k in range(8):
                    tr(p[:, k * 128:(k + 1) * 128], xnB[:, k * 128:(k + 1) * 128], ident_b[:], last=(k == 7))
                cp("act", hTB[:], Vw(p.t[:].rearrange("p (k t) -> p k t", k=8), p.b))

            def projB(c0, c1):
                p = next_mm()
                n = c1 - c0
                for k in range(8):
                    mm(p[:, 0:n], hTB[:, k, :], W_B[:, k, c0:c1], start=(k == 0), stop=(k == 7), last=(k == 7))
                return p

            def q_tile(x_rows, cs_rows, rw_src, qb):
                loadB(x_rows)
                dma("sp", cstB[:], cs_rows)
                p = projB(0, 256)
                act(cqn[:], p[:, 0:256], AF.Square, scale=1.0 / 16.0, accum=ssB[:, 3:4])
                act(ssB[:, 3:4], ssB[:, 3:4], AF.Sqrt, bias=1e-6)
                recip(ssB[:, 3:4], ssB[:, 3:4])
                ts("dve", cqn[:], p[:, 0:256], ssB[:, 3:4], None, ALU.mult)
                pt_ = next_tr()
                for k in range(2):
                    tr(pt_[:, k * 128:(k + 1) * 128], cqn[:, k * 128:(k + 1) * 128], ident_b[:], last=(k == 1))
                cp("act", cqT[:], Vw(pt_.t[:, 0:256].rearrange("p (k t) -> p k t", k=2), pt_.b))
                for half in range(2):
                    p = next_mm()
                    for k in range(2):
                        mm(p[:, 0:384], cqT[:, k, :], W_uq[:, k, half * 384:(half + 1) * 384], start=(k == 0), stop=(k == 1), last=(k == 1))
                    cp(evac_eng(), Vw(q_f.t[:, half * 4:(half + 1) * 4, :], q_f.b),
                       Vw(p.t[:, 0:384].rearrange("p (h d) -> p h d", h=4), p.b))
                for h in range(8):
                    e_ = "dve" if h % 2 else "pool"
                    tt(e_, rq[:, h, 0:32], q_f[:, h, 64:96], cstB[:, 0:32], ALU.mult)
                    tt(e_, rq[:, h, 32:64], q_f[:, h, 64:96], cstB[:, 32:64], ALU.mult)
                    tt(e_, q_f[:, h, 64:80], rq[:, h, 0:16], rq[:, h, 48:64], ALU.subtract)
                    tt(e_, q_f[:, h, 80:96], rq[:, h, 16:32], rq[:, h, 32:48], ALU.add)
                cp("act", q_b[:], q_f[:])
                for hh in range(2):
                    pt_ = next_tr()
                    for h4 in range(4):
                        tr(pt_[0:96, h4 * 128:(h4 + 1) * 128], q_b[:, hh * 4 + h4, :], ident_b[:], last=(h4 == 3))
                    cp(evac_eng(), Vw(qT.t[:, hh * 4:(hh + 1) * 4, qb * 128:(qb + 1) * 128], qT.b),
                       Vw(pt_.t[0:96, 0:512].rearrange("p (h t) -> p h t", h=4), pt_.b))
                p = projB(256, 768)
                act(gate[:, qb, 0:512], p[:, 0:512], AF.Silu)
                p = projB(768, 1280)
                act(gate[:, qb, 512:768], p[:, 0:256], AF.Silu)
                cp("dve", qx_b[:], p[:, 256:512])
                tt("dve", mixed[:, qb, 512:768], gate[:, qb, 512:768], rw_src, ALU.mult)
                p = projB(1280, 1536)
                act(gate[:, qb, 768:1024], p[:, 0:256], AF.Silu)
                pt_ = next_tr()
                for h in range(4):
                    tr(pt_[0:64, h * 128:(h + 1) * 128], qx_b[:, h * 64:(h + 1) * 64], ident_b[:], last=(h == 3))
                cp(evac_eng(), Vw(qxT.t[:, :, qb * 128:(qb + 1) * 128], qxT.b),
                   Vw(pt_.t[0:64, 0:512].rearrange("p (h t) -> p h t", h=4), pt_.b))

            acc_i = [0]

            def attn_head(kv_list, kT_of, V_of, q_rhs, scale, mask_of, out_col):
                acc_i[0] ^= 1
                acc = ps_acc[acc_i[0]]
                nkv = len(kv_list)
                for n_, (j, c0) in enumerate(kv_list):
                    p = next_mm()
                    ml = mask_of(j)
                    mm(p[:, c0:NQ], kT_of(j), Vw(q_rhs.ap[:, c0:NQ], q_rhs.b), start=True, stop=(len(ml) == 0), last=(len(ml) == 0))
                    for mi, (qb_, mcol) in enumerate(ml):
                        mm(p[:, qb_ * 128:(qb_ + 1) * 128], ident_b[:], masks_b[:, mcol * 128:(mcol + 1) * 128],
                           start=False, stop=(mi == len(ml) - 1), last=(mi == len(ml) - 1))
                    pt_t = Pt[n_ % 2]
                    act(pt_t[:, c0:NQ], p[:, c0:NQ], AF.Exp, scale=scale)
                    mm(acc[0:65, c0:NQ], V_of(j), pt_t[:, c0:NQ], start=(n_ == 0), stop=(n_ == nkv - 1), last=True)
                cp("act", OT[:], acc[0:65, 0:NQ])
                po = next_mm()
                for qb_ in range(GQ):
                    tr(po[:, qb_ * 65:(qb_ + 1) * 65], OT[:, qb_ * 128:(qb_ + 1) * 128], ident_f[0:65, 0:65], last=(qb_ == GQ - 1))
                S.op("dve", lambda e: e.reciprocal(out=rden.t[:, 0:GQ], in_=po.t[:, 0:GQ * 65].rearrange("p (q d) -> p q d", q=GQ)[:, :, 64]),
                     reads=[po.b], writes=[rden.b])
                for qb_ in range(GQ):
                    stt("dve", mixed[:, qb_, out_col:out_col + 64], po[:, qb_ * 65:qb_ * 65 + 64], rden[:, qb_:qb_ + 1],
                        gate[:, qb_, out_col:out_col + 64], ALU.mult, ALU.mult)

            def out_tile(x_rows, qb, y_rows):
                pt_ = next_tr()
                for k in range(8):
                    tr(pt_[:, k * 128:(k + 1) * 128], mixed[:, qb, k * 128:(k + 1) * 128], ident_b[:], last=(k == 7))
                cp("act", mT[:], Vw(pt_.t[:].rearrange("p (k t) -> p k t", k=8), pt_.b))
                dma("sp", xtB[:], x_rows)
                for half in range(2):
                    p = next_mm()
                    for k in range(8):
                        mm(p[:, :], mT[:, k, :], W_out[:, k, half * 512:(half + 1) * 512], start=(k == 0), stop=(k == 7), last=(k == 7))
                    tt("dve", xtB[:, half * 512:(half + 1) * 512], xtB[:, half * 512:(half + 1) * 512], p[:, :], ALU.add)
                act(xnB[:], xtB[:], AF.Square, scale=1.0 / 32.0, accum=ssB[:, 4:5])
                act(ssB[:, 4:5], ssB[:, 4:5], AF.Sqrt, bias=1e-6)
                recip(ssB[:, 4:5], ssB[:, 4:5])
                stt("dve", xtB[:], xtB[:], ssB[:, 4:5], fing_bc[:], ALU.mult, ALU.mult)
                if isinstance(y_rows, list):
                    for (dst, r0, r1) in y_rows:
                        dma("sp", dst, xtB[r0:r1, :])
                else:
                    dma("sp", y_rows, xtB[:])

            for g in range(NG if STAGE >= 3 else 0):
                for qb in range(GQ):
                    m = g * GQ + qb
                    q_tile(D["x_own"][m * 128:(m + 1) * 128, :], D["cs_own"][m * 128:(m + 1) * 128, :], rw_own[:, m, :], qb)
                jmax = 2 * (g * GQ + GQ - 1) + 1
                kv_list = []
                for j in range(jmax + 1):
                    qb0 = max(0, -(-(j - 1 - 2 * g * GQ) // 2))
                    kv_list.append((j, qb0 * 128))

                def mask_of(j, g=g):
                    ml = []
                    for qb_ in range(GQ):
                        m_ = g * GQ + qb_
                        if j == 2 * m_:
                            ml.append((qb_, 0))
                        elif j == 2 * m_ + 1:
                            ml.append((qb_, 1))
                    return ml

                for h in range(8):
                    attn_head(kv_list, lambda j, h=h: kT[:, h, j * 128:(j + 1) * 128], lambda j, h=h: Vaug[:, j, h, :],
                              qT[:, h, :], MLA_SCALE, mask_of, h * 64)
                for h in range(4):
                    attn_head([(0, 0), (1, 0)], lambda j, h=h: mkT[:, h, j * 128:(j + 1) * 128], lambda j, h=h: mv_aug[:, j, h, :],
                              qxT[:, h, :], X_SCALE, lambda j: [], 768 + h * 64)
                for qb in range(GQ):
                    m = g * GQ + qb
                    out_tile(D["x_own"][m * 128:(m + 1) * 128, :], qb, O["y_own"][m * 128:(m + 1) * 128, :])

            if STAGE >= 5:
                phS = contextlib.ExitStack()
                S.barrier()
                pools = {128: [Vaug.t[:].rearrange("p a b c -> p (a b c)"), 0, 32 * 8 * 65],
                         96: [kT.t[:].rearrange("p a b -> p (a b)"), 0, 8 * 4096]}

                def carve(parts, shape, dt):
                    key = 128 if parts > 96 else 96
                    flat, off, cap = pools[key]
                    n = int(np.prod(shape)) * (1 if dt == BF16 else 2)
                    n += n % 2
                    assert off + n <= cap, (parts, shape, off, n, cap)
                    ap = flat[0:parts, off:off + n]
                    pools[key][1] = off + n
                    if dt != BF16:
                        ap = ap.bitcast(dt)
                    ap = ap[:, 0:int(np.prod(shape))]
                    if len(shape) == 2:
                        ap = ap.rearrange("p (a b) -> p a b", a=shape[0])
                    elif len(shape) == 3:
                        ap = ap.rearrange("p (a b c) -> p a b c", a=shape[0], b=shape[1])
                    return Tl(ap)

                WukT = carve(64, [8, 128], BF16)
                for hh in range(2):
                    pt_ = next_tr()
                    for h4 in range(4):
                        h = hh * 4 + h4
                        tr(pt_[0:64, h4 * 128:(h4 + 1) * 128], W_uk[:, h * 64:(h + 1) * 64], ident_b[:], last=(h4 == 3))
                    cp(evac_eng(), Vw(WukT.t[:, hh * 4:(hh + 1) * 4, :], WukT.b),
                       Vw(pt_.t[0:64, 0:512].rearrange("p (h t) -> p h t", h=4), pt_.b))
                ones_b = carve(128, [128], BF16)
                memset("pool", ones_b[:], 1.0)
                mk_sf = sb([128, 256], F32, phS)
                msk_b = carve(128, [4, 64], BF16)
                dma("sp", mk_sf[:], D["masks_s"])
                cp("dve", Vw(msk_b.t[:].rearrange("p q n -> p (q n)"), msk_b.b), mk_sf[:])
                ptb = sb([128, 64], I32, phS)
                ptf = sb([128, 64], F32, phS)
                idx_t = [sb([128, 64], I32, phS) for _ in range(2)]
                io_t = sb([128, 1], F32, phS)
                dma("sp", io_t[:], D["iota_p"])
                pt_rows = D["page_table"].rearrange("o (s p) -> (o s) p", p=64)
                qpeT = carve(32, [8, 128], BF16)
                qlat_all = carve(128, [8, 128], BF16)
                qlat_s = carve(128, [64], BF16)
                qpe_s = carve(32, [64], BF16)
                pg_f = [sb([128, 4, 160], F32, phS)] * 2
                pg_b = [carve(128, [4, 160], BF16) for _ in range(2)]
                kT_pg = [carve(128, [4, 128], BF16) for _ in range(2)]
                kpT_pg = [carve(32, [4, 128], BF16) for _ in range(2)]
                Pt_s = [carve(128, [256], BF16) for _ in range(2)]
                rdn = sb([128, 64], F32, phS)
                olat = carve(128, [8, 128], BF16)
                memset("pool", olat[:], 0.0)
                mk_sb = carve(128, [2, 256], BF16)
                mkTS = carve(64, [4, 256], BF16)
                mvS_aug = carve(128, [2, 4, 65], BF16)
                memset("pool", mvS_aug[:], 1.0)
                Pm = [carve(128, [8, 128], BF16) for _ in range(4)]
                for t_ in Pm:
                    memset("pool", t_[:], 0.0)
                rd4 = sb([128, 4], F32, phS)
                zero_b = carve(128, [512], BF16)
                memset("pool", zero_b[:], 0.0)
                ck_v = D["cache_ckv"]
                kp_v = D["cache_kpe"]
                for tau in range(4):
                    q_tile(D["x_smp"][tau * 128:(tau + 1) * 128, :], D["cs_smp"][tau * 128:(tau + 1) * 128, :], rw_smp[:, tau, :], 0)
                    for hh in range(2):
                        pt_ = next_tr()
                        for h4 in range(4):
                            tr(pt_[0:32, h4 * 128:(h4 + 1) * 128], q_b[:, hh * 4 + h4, 64:96], ident_b[:], last=(h4 == 3))
                        cp(evac_eng(), Vw(qpeT.t[:, hh * 4:(hh + 1) * 4, :], qpeT.b),
                           Vw(pt_.t[0:32, 0:512].rearrange("p (h t) -> p h t", h=4), pt_.b))
                    for hh in range(2):
                        p = next_mm()
                        for h4 in range(4):
                            h = hh * 4 + h4
                            mm(p[:, h4 * 128:(h4 + 1) * 128], WukT[:, h, :], qT[0:64, h, 0:128], last=(h4 == 3))
                        cp(evac_eng(), Vw(qlat_all.t[:, hh * 4:(hh + 1) * 4, :], qlat_all.b),
                           Vw(p.t[:, :].rearrange("p (h t) -> p h t", h=4), p.b))
                    for q in range(4):
                        sq = 4 * tau + q
                        cols = slice(32 * q, 32 * q + 8)
                        cp("dve", Vw(qlat_s.t[:].rearrange("p (h t) -> p h t", h=8), qlat_s.b), Vw(qlat_all.t[:, :, cols], qlat_all.b))
                        cp("pool", Vw(qpe_s.t[:].rearrange("p (h t) -> p h t", h=8), qpe_s.b), Vw(qpeT.t[:, :, cols], qpeT.b))
                        acc = ps_acc[0]
                        ix = idx_t[sq % 2]
                        dma("sp", ptb[:], pt_rows[sq, :].partition_broadcast(128))
                        cp("dve", ptf[:], ptb[:])
                        ts("dve", ptf[:], ptf[:], 128.0, io_t[:, 0:1], ALU.mult, ALU.add)
                        cp("dve", ix[:], ptf[:])
                        mm(acc[:, 0:128], zero_b[:, 0:128], zero_b[:, 0:128], start=True, stop=False, last=True)
                        for step in range(16):
                            b_ = step % 2
                            for k in range(4):
                                pgi = step * 4 + k
                                S.dma_gather(pg_f[b_].t[:, k, 0:128], ck_v, ix.t[:, pgi:pgi + 1], reads=[ix.b], writes=[pg_f[b_].b])
                                S.dma_gather(pg_f[b_].t[:, k, 128:160], kp_v, ix.t[:, pgi:pgi + 1], reads=[ix.b], writes=[pg_f[b_].b])
                            cp("pool", pg_b[b_][:], pg_f[b_][:])
                            pt_ = next_tr()
                            for k in range(4):
                                tr(pt_[:, k * 128:(k + 1) * 128], pg_b[b_][:, k, 0:128], ident_b[:], last=False)
                            for k in range(4):
                                tr(pt_[0:32, 512 + k * 128:512 + (k + 1) * 128], pg_b[b_][:, k, 128:160], ident_b[:], last=(k == 3))
                            cp("act", kT_pg[b_][:], Vw(pt_.t[:, 0:512].rearrange("p (k t) -> p k t", k=4), pt_.b))
                            cp("dve", kpT_pg[b_][:], Vw(pt_.t[0:32, 512:1024].rearrange("p (k t) -> p k t", k=4), pt_.b))
                            p = next_mm()
                            for k in range(4):
                                mm(p[:, k * 64:(k + 1) * 64], kT_pg[b_][:, k, :], qlat_s[:], start=True, stop=False, last=False)
                                mm(p[:, k * 64:(k + 1) * 64], kpT_pg[b_][:, k, :], qpe_s[:], start=False, stop=True, last=(k == 3))
                            act(Pt_s[b_][:], p[:, 0:256], AF.Exp, scale=MLA_SCALE)
                            for k in range(4):
                                mm(acc[:, 0:64], pg_b[b_][:, k, 0:128], Pt_s[b_][:, k * 64:(k + 1) * 64], start=False, stop=False, last=False)
                                mm(acc[:, 64:128], ones_b[:], Pt_s[b_][:, k * 64:(k + 1) * 64], start=False, stop=False, last=(k == 3))
                        p = next_mm()
                        mm(p[:, 0:64], ckvTS[:, tau, :], qlat_s[:], start=True, stop=False, last=False)
                        mm(p[:, 0:64], kpeTS[:, tau, :], qpe_s[:], start=False, stop=False, last=False)
                        mm(p[:, 0:64], ident_b[:], msk_b[:, q, :], start=False, stop=True, last=True)
                        act(Pt_s[0][:, 0:64], p[:, 0:64], AF.Exp, scale=MLA_SCALE)
                        mm(acc[:, 0:64], ckvS_b[:, tau, :], Pt_s[0][:, 0:64], start=False, stop=True, last=False)
                        mm(acc[:, 64:128], ones_b[:], Pt_s[0][:, 0:64], start=False, stop=True, last=True)
                        recip(rdn[:], acc[:, 64:128])
                        tt("dve", Vw(olat.t[:, :, cols], olat.b), Vw(acc.t[:, 0:64].rearrange("p (h t) -> p h t", h=8), acc.b),
                           Vw(rdn.t[:].rearrange("p (h t) -> p h t", h=8), rdn.b), ALU.mult)
                    po = next_mm()
                    for h in range(8):
                        mm(po[:, h * 64:(h + 1) * 64], olat[:, h, :], W_uv[:, h * 64:(h + 1) * 64], last=(h == 7))
                    tt("dve", mixed[:, 0, 0:512], po[:, :], gate[:, 0, 0:512], ALU.mult)
                    pm = ps_acc[1]
                    mm(pm[:, 0:260], zero_b[:, 0:128], zero_b[:, 0:260], start=True, stop=False, last=True)
                    for q in range(4):
                        sq = 4 * tau + q
                        cols = slice(32 * q, 32 * q + 8)
                        for j in range(2):
                            dma("sp", mk_sf[:], D["cache_mem_k"][sq, j * 128:(j + 1) * 128, :])
                            cp("pool", mk_sb[:, j, :], mk_sf[:])
                            dma("sp", mk_sf[:], D["cache_mem_v"][sq, j * 128:(j + 1) * 128, :])
                            cp("pool", Vw(mvS_aug.t[:, j, :, 0:64], mvS_aug.b), Vw(mk_sf.t[:].rearrange("p (h d) -> p h d", h=4), mk_sf.b))
                        for j in range(2):
                            pt_ = next_tr()
                            for h in range(4):
                                tr(pt_[0:64, h * 128:(h + 1) * 128], mk_sb[:, j, h * 64:(h + 1) * 64], ident_b[:], last=(h == 3))
                            cp(evac_eng(), Vw(mkTS.t[:, :, j * 128:(j + 1) * 128], mkTS.b),
                               Vw(pt_.t[0:64, 0:512].rearrange("p (h t) -> p h t", h=4), pt_.b))
                        p = next_mm()
                        for j in range(2):
                            for h in range(4):
                                c0 = (j * 4 + h) * 8
                                mm(p[:, c0:c0 + 8], mkTS[:, h, j * 128:(j + 1) * 128], qxT[:, h, cols], last=(j == 1 and h == 3))
                        act(Vw(Pm[q].t[:, :, cols], Pm[q].b), Vw(p.t[:, 0:64].rearrange("p (g t) -> p g t", g=8), p.b), AF.Exp, scale=X_SCALE)
                        for h in range(4):
                            for j in range(2):
                                mm(pm[:, h * 65:(h + 1) * 65], Pm[q][:, j * 4 + h, :], mvS_aug[:, j, h, :],
                                   start=False, stop=(q == 3 and j == 1), last=(h == 3 and j == 1))
                    S.op("dve", lambda e: e.tensor_scalar(out=rd4.t[:, 0:4], in0=pm.t[:, 0:260].rearrange("p (h d) -> p h d", h=4)[:, :, 64],
                                                          scalar1=1e-30, scalar2=None, op0=ALU.add),
                         reads=[pm.b], writes=[rd4.b])
                    recip(rd4[:], rd4[:])
                    for h in range(4):
                        stt("dve", mixed[:, 0, 768 + h * 64:768 + (h + 1) * 64], pm[:, h * 65:h * 65 + 64], rd4[:, h:h + 1],
                            gate[:, 0, 768 + h * 64:768 + (h + 1) * 64], ALU.mult, ALU.mult)
                    out_tile(D["x_smp"][tau * 128:(tau + 1) * 128, :], 0,
                             [(O["y_smp"][(4 * tau + q) * 8:(4 * tau + q + 1) * 8, :], 32 * q, 32 * q + 8) for q in range(4)])
                S.barrier()
                phS.close()
            S.barrier()
            phB2.close()
            phB.close()
        S.finish()
    return nc


_NC_CACHE = {}


def _rope_table(pos):
    half = 16
    inv = (1.0 / (np.float32(10000.0) ** (np.arange(half, dtype=np.float32) * np.float32(2.0 / 32)))).astype(np.float32)
    ang = pos.astype(np.float32)[:, None] * inv[None, :]
    c = np.cos(ang).astype(np.float32)
    s = np.sin(ang).astype(np.float32)
    return np.concatenate([c, c, s, s], axis=1).astype(np.float32)


def _tri_consts(C):
    idx = np.arange(128)
    same = (idx[:, None] // C) == (idx[None, :] // C)
    s_le_t = (idx[:, None] <= idx[None, :]) & same
    s_lt_t = (idx[:, None] < idx[None, :]) & same
    s_gt_t = (idx[:, None] > idx[None, :]) & same
    return np.concatenate([s_le_t, same, s_lt_t, s_le_t, s_gt_t], axis=1).astype(np.float32)


def kernel(**inputs):
    f = lambda a: np.ascontiguousarray(np.asarray(a))
    inp = {k: f(v) for k, v in inputs.items()}
    if "nc" not in _NC_CACHE:
        _NC_CACHE["nc"] = build_nc()
    nc = _NC_CACHE["nc"]
    xp = inp["x_prompt"]
    xs = inp["x_sample"].reshape(128 * 8, 1024)
    ident = np.eye(128, dtype=np.float32)
    idx = np.arange(128)
    tri_mask = np.where(idx[:, None] > idx[None, :], NEG, 0.0).astype(np.float32)
    full_mask = np.full((128, 128), NEG, np.float32)
    zero_mask = np.zeros((128, 128), np.float32)
    cs_all = _rope_table(np.arange(4096))
    cs8 = _rope_table(8192 + np.arange(8))
    cs_smp = np.zeros((16, 32, 64), np.float32)
    cs_smp[:, 0:8] = cs8[None]
    cs_smp = cs_smp.reshape(512, 64)
    r_ = np.arange(128)
    ms = np.full((128, 4, 8, 8), NEG, np.float32)
    for q_ in range(4):
        for t_ in range(8):
            ok = (r_ // 32 == q_) & (r_ % 32 <= t_) & (r_ % 32 < 8)
            ms[ok, q_, :, t_] = 0.0
    masks_s = ms.reshape(128, 256)
    shared = dict(
        ident=ident, tri64=_tri_consts(64), tri8=_tri_consts(8),
        cache_ckv=inp["cache_ckv"].reshape(10240 * 128, 128)[:NPOOL * 128], cache_kpe=inp["cache_kpe"].reshape(10240 * 128, 32)[:NPOOL * 128],
        iota_p=np.arange(128, dtype=np.float32).reshape(128, 1),
        ln_g=f(inp["ln_g"].reshape(8, 128).T), w_in=inp["w_in"].reshape(1024, 2592), q_norm_g=f(inp["q_norm_g"].reshape(2, 128).T),
        kv_norm_g=inp["kv_norm_g"].reshape(128), w_uq=inp["w_uq"].reshape(256, 768), w_uk=inp["w_uk"].reshape(128, 512),
        w_uv=inp["w_uv"].reshape(128, 512), shift_mu=inp["shift_mu"].reshape(896), w0=inp["w0"].reshape(256),
        w_up=inp["w_up"].reshape(64, 256), a0=inp["a0"].reshape(256), a_up=inp["a_up"].reshape(64, 256),
        k_k=inp["k_k"].reshape(256), k_a=inp["k_a"].reshape(256), r_k=inp["r_k"].reshape(256),
        lnx_g=inp["lnx_g"].reshape(256), lnx_b=inp["lnx_b"].reshape(256), mem_norm_g=f(inp["mem_norm_g"].reshape(8, 128).T),
        w_mem_kv=inp["w_mem_kv"].reshape(1024, 512), w_out=inp["w_out"].reshape(1024, 1024),
        final_g=inp["final_g"].reshape(1024), cs_seq=cs_all, cs_smp=cs_smp, masks_s=masks_s,
    )
    in_maps = []
    for c in range(8):
        b, j = c // 2, c % 2
        xb = xp[b].reshape(32, 128, 1024)
        m = dict(shared)
        m["x_seq"] = xp[b]
        m["x_own"] = f(xb[j::2].reshape(2048, 1024))
        xpad = np.zeros((16, 32, 1024), np.float32)
        xpad[:, 0:8] = xs[c * 128:(c + 1) * 128].reshape(16, 8, 1024)
        m["x_smp"] = xpad.reshape(512, 1024)
        m["mem"] = inp["mem_prompt"][b]
        m["cs_own"] = f(cs_all.reshape(32, 128, 64)[j::2].reshape(2048, 64))
        m["masks"] = f(np.concatenate([tri_mask, full_mask] if j == 0 else [zero_mask, tri_mask], axis=1))
        m["sel"] = f(np.tile(np.array([[1.0, 0.0]] if j == 0 else [[0.0, 1.0]], np.float32), (128, 1)))
        pt_c = inp["page_table"][c * 16:(c + 1) * 16].reshape(1, 1024).astype(np.int32)
        m["page_table"] = f(pt_c % NPOOL) if KDEV else f(pt_c)
        m["state_wkv"] = f(inp["state_wkv"][0, c * 16:(c + 1) * 16])
        m["state_shift"] = f(inp["state_shift"][0, c * 16:(c + 1) * 16])
        m["cache_mem_k"] = f(inp["cache_mem_k"][0, c * 16:(c + 1) * 16].reshape(16, 256, 256))
        m["cache_mem_v"] = f(inp["cache_mem_v"][0, c * 16:(c + 1) * 16].reshape(16, 256, 256))
        in_maps.append(m)
    res = run_bass_kernel_spmd(nc, in_maps, core_ids=list(range(8)))
    R = res.results
    y_prompt = np.zeros((4, 32, 128, 1024), np.float32)
    for c in range(8):
        y_prompt[c // 2, (c % 2)::2] = R[c]["y_own"].reshape(16, 128, 1024)
    y_prompt = y_prompt.reshape(4, 4096, 1024)
    y_sample = np.concatenate([R[c]["y_smp"] for c in range(8)], 0).reshape(128, 8, 1024)
    ev = [R[2 * b] for b in range(4)]
    ckv_p = np.stack([r["ckv_seq"] for r in ev])[None]
    kpe_p = np.stack([r["kpe_seq"] for r in ev])[None]
    wkv_p = np.stack([r["wkv_p"] for r in ev])[None]
    shift_p = np.stack([r["shift_p"].reshape(896) for r in ev])[None]
    memk = np.stack([r["memk"].reshape(256, 4, 64) for r in ev])[None]
    memv = np.stack([r["memv"].reshape(256, 4, 64) for r in ev])[None]
    ckv_s = np.concatenate([R[c]["ckv_s"] for c in range(8)], 0).reshape(1, 128, 8, 128)
    kpe_s = np.concatenate([R[c]["kpe_s"] for c in range(8)], 0).reshape(1, 128, 8, 32)
    wkv_s = np.concatenate([R[c]["wkv_s"] for c in range(8)], 0)[None]
    shift_s = np.concatenate([R[c]["shift_s"] for c in range(8)], 0)[None]
    outs = (y_prompt, y_sample, ckv_p, kpe_p, wkv_p, shift_p, memk, memv, ckv_s, kpe_s, wkv_s, shift_s)
    return tuple(np.ascontiguousarray(o, dtype=np.float32) for o in outs)
```

```python
import contextlib
import os
import numpy as np
import concourse.bass as bass
import concourse.mybir as mybir
from concourse.bass_utils import run_bass_kernel_spmd

F32 = mybir.dt.float32
BF16 = mybir.dt.bfloat16
I32 = mybir.dt.int32
AF = mybir.ActivationFunctionType
ALU = mybir.AluOpType
AX = mybir.AxisListType

KDEV = os.environ.get("KDEV") == "1"
NPOOL = 64 if KDEV else 10240
KSUB = int(os.environ.get("KSUB", "9"))
KP = int(os.environ.get("KP", "9"))
KQ = int(os.environ.get("KQ", "9"))
KR = int(os.environ.get("KR", "9"))
KT = int(os.environ.get("KT", "3"))
STAGE = int(os.environ.get("KSTAGE", "5"))
NEG = -30000.0
MLA_SCALE = 96 ** -0.5
X_SCALE = 64 ** -0.5


class Buf:
    __slots__ = ("w", "r", "excl")

    def __init__(self):
        self.w = None
        self.r = []
        self.excl = False


class Vw:
    __slots__ = ("ap", "b")

    def __init__(self, ap, b):
        self.ap = ap
        self.b = b


class Tl:
    def __init__(self, t):
        self.t = t
        self.b = Buf()

    def __getitem__(self, idx):
        return Vw(self.t[idx], self.b)


class Sched:
    EPOCH = 3000

    def __init__(self, nc, n_dma_sems=32):
        self.nc = nc
        self.eng = {"pe": nc.tensor, "dve": nc.vector, "act": nc.scalar,
                    "pool": nc.gpsimd, "sp": nc.sync}
        self.sem = {}
        self.cnt = {}
        self.nsem = 0
        self.keep = []
        for e in self.eng:
            self._new_sem(e)
        self.waited = {e: {} for e in self.eng}
        self.pending = {}
        self.dma_sems = []
        for i in range(n_dma_sems):
            self.dma_sems.append([nc.alloc_semaphore(f"dq{i}"), 0])
        self.dma_rr = 0
        self.gq_sems = [[nc.alloc_semaphore(f"gq{i}"), 0] for i in range(40)]
        self.gq_rr = 0
        self.n_inst = 0

    def _new_sem(self, e):
        self.sem[e] = self.nc.alloc_semaphore(f"s_{e}_{self.nsem}")
        self.keep.append(self.sem[e])
        self.nsem += 1
        self.cnt[e] = 0

    def _wait(self, e, deps):
        best = {}
        for (s, v) in deps:
            k = id(s)
            if k not in best or best[k][1] < v:
                best[k] = (s, v)
        for k, (s, v) in best.items():
            if self.waited[e].get(k, 0) >= v:
                continue
            self.eng[e].wait_ge(s, v)
            self.waited[e][k] = v

    def _record(self, ev, reads, writes):
        for b in writes:
            b.w = ev
            b.r = []
        for b in reads:
            if b.w is ev:
                continue
            b.r.append(ev)
            if len(b.r) > 8:
                best = {}
                for (s, v) in b.r:
                    k = id(s)
                    if k not in best or best[k][1] < v:
                        best[k] = (s, v)
                b.r = list(best.values())

    def op(self, e, fn, reads=(), writes=(), pe_acc=False, inc=True):
        reads = [b for b in reads if b is not None]
        writes = [b for b in writes if b is not None]
        own = id(self.sem[e])
        deps = []
        for b in reads:
            if b.w is not None:
                deps.append(b.w)
            if b.excl:
                deps.extend(ev_ for ev_ in b.r if id(ev_[0]) != own)
        for b in writes:
            if b.w is not None and not (pe_acc and id(b.w[0]) == own):
                deps.append(b.w)
            deps.extend(b.r)
        self._wait(e, deps)
        if self.cnt[e] >= self.EPOCH and not self.pending.get(e, False):
            self._new_sem(e)
        ins = fn(self.eng[e])
        self.n_inst += 1
        if inc:
            self.cnt[e] += 1
            ins.then_inc(self.sem[e], 1)
            ev = (self.sem[e], self.cnt[e])
            self.pending[e] = False
        else:
            ev = (self.sem[e], self.cnt[e] + 1)
            self.pending[e] = True
        self._record(ev, reads, writes)
        return ev

    def dma(self, e, out, in_, reads=(), writes=(), **kw):
        reads = [b for b in reads if b is not None]
        writes = [b for b in writes if b is not None]
        deps = []
        for b in reads:
            if b.w is not None:
                deps.append(b.w)
        for b in writes:
            if b.w is not None:
                deps.append(b.w)
            deps.extend(b.r)
        slot = self.gq_sems[self.gq_rr]
        self.gq_rr = (self.gq_rr + 1) % len(self.gq_sems)
        if slot[1] > 0:
            deps.append((slot[0], slot[1]))
        self._wait(e, deps)
        ins = self.eng[e].dma_start(out=out, in_=in_, **kw)
        slot[1] += 16
        ins.then_inc(slot[0], 16)
        ev = (slot[0], slot[1])
        self._record(ev, reads, writes)
        self.n_inst += 1
        return ev

    def dma_gather(self, out, in_, idx, reads=(), writes=()):
        e = "pool"
        reads = [b for b in reads if b is not None]
        writes = [b for b in writes if b is not None]
        deps = []
        for b in reads:
            if b.w is not None:
                deps.append(b.w)
        for b in writes:
            if b.w is not None:
                deps.append(b.w)
            deps.extend(b.r)
        slot = self.gq_sems[self.gq_rr]
        self.gq_rr = (self.gq_rr + 1) % len(self.gq_sems)
        if slot[1] > 0:
            deps.append((slot[0], slot[1]))
        self._wait(e, deps)
        ins = self.nc.gpsimd.indirect_dma_start(out=out, out_offset=None, in_=in_,
                                                in_offset=bass.IndirectOffsetOnAxis(ap=idx, axis=0))
        slot[1] += 16
        ins.then_inc(slot[0], 16)
        ev = (slot[0], slot[1])
        self._record(ev, reads, writes)
        self.n_inst += 1
        return ev

    def wait_bufs(self, e, bufs):
        deps = []
        for b in bufs:
            if b.w is not None:
                deps.append(b.w)
        self._wait(e, deps)

    def barrier(self):
        deps = []
        for s, v in self.dma_sems + self.gq_sems:
            if v > 0:
                deps.append((s, v))
        for e in self.eng:
            if self.cnt[e] > 0:
                deps.append((self.sem[e], self.cnt[e]))
        for e in self.eng:
            self._wait(e, deps)

    def finish(self):
        deps = []
        for s, v in self.dma_sems + self.gq_sems:
            if v > 0:
                deps.append((s, v))
        for e in self.eng:
            if self.cnt[e] > 0:
                deps.append((self.sem[e], self.cnt[e]))
        self._wait("sp", deps)


IN_SPECS = [
    ("x_seq", [4096, 1024], F32), ("x_own", [2048, 1024], F32), ("x_smp", [512, 1024], F32),
    ("mem", [256, 1024], F32),
    ("cs_seq", [4096, 64], F32), ("cs_own", [2048, 64], F32), ("cs_smp", [512, 64], F32), ("masks_s", [128, 256], F32),
    ("masks", [128, 256], F32), ("sel", [128, 2], F32), ("ident", [128, 128], F32),
    ("tri64", [128, 640], F32), ("tri8", [128, 640], F32),
    ("cache_ckv", [NPOOL * 128, 128], F32), ("cache_kpe", [NPOOL * 128, 32], F32), ("iota_p", [128, 1], F32),
    ("page_table", [1, 1024], I32),
    ("state_wkv", [16, 4, 64, 64], F32), ("state_shift", [16, 896], F32),
    ("cache_mem_k", [16, 256, 256], F32), ("cache_mem_v", [16, 256, 256], F32),
    ("ln_g", [128, 8], F32), ("w_in", [1024, 2592], F32), ("q_norm_g", [128, 2], F32),
    ("kv_norm_g", [128], F32), ("w_uq", [256, 768], F32), ("w_uk", [128, 512], F32),
    ("w_uv", [128, 512], F32), ("shift_mu", [896], F32), ("w0", [256], F32), ("w_up", [64, 256], F32),
    ("a0", [256], F32), ("a_up", [64, 256], F32), ("k_k", [256], F32), ("k_a", [256], F32),
    ("r_k", [256], F32), ("lnx_g", [256], F32), ("lnx_b", [256], F32), ("mem_norm_g", [128, 8], F32),
    ("w_mem_kv", [1024, 512], F32), ("w_out", [1024, 1024], F32), ("final_g", [1024], F32),
]
OUT_SPECS = [
    ("y_own", [2048, 1024]), ("y_smp", [128, 1024]), ("ckv_seq", [4096, 128]), ("kpe_seq", [4096, 32]),
    ("wkv_p", [4, 64, 64]), ("shift_p", [1, 896]), ("memk", [256, 256]), ("memv", [256, 256]),
    ("ckv_s", [128, 128]), ("kpe_s", [128, 32]), ("wkv_s", [16, 4, 64, 64]), ("shift_s", [16, 896]),
]


def build_nc():
    nc = bass.Bass("TRN2", target_bir_lowering=False)
    D = {}
    for name, shape, dt in IN_SPECS:
        D[name] = nc.dram_tensor(name, shape, dt, kind="ExternalInput").ap()
    O = {}
    for name, shape in OUT_SPECS:
        O[name] = nc.dram_tensor(name, shape, F32, kind="ExternalOutput").ap()
    S = Sched(nc)
    es = contextlib.ExitStack()
    uid = [0]

    def sb(shape, dt, stack=None):
        uid[0] += 1
        return Tl((stack or es).enter_context(nc.sbuf_tensor(f"sb{uid[0]}", shape, dt)))

    def ps(shape, dt):
        uid[0] += 1
        t = Tl(es.enter_context(nc.psum_tensor(f"ps{uid[0]}", shape, dt)))
        t.b.excl = True
        return t

    def bufs(*vs):
        return [v.b for v in vs if isinstance(v, Vw)]

    def apof(v):
        return v.ap if isinstance(v, Vw) else v

    def dma(e, out, in_):
        o_ap = apof(out)
        i_ap = apof(in_)
        S.dma(e, o_ap, i_ap, reads=bufs(in_), writes=bufs(out))

    def act(out, in_, func, bias=0.0, scale=1.0, accum=None, eng="act"):
        kw = {}
        if accum is not None:
            kw["accum_out"] = accum.ap
        S.op(eng, lambda e: e.activation(out=out.ap, in_=in_.ap, func=func, bias=apof(bias),
                                         scale=apof(scale), **kw),
             reads=bufs(in_, bias, scale), writes=bufs(out, accum))

    def ts(eng, out, in0, s1, s2, op0, op1=None, accum=None):
        kw = {}
        if op1 is not None:
            kw["op1"] = op1
        if accum is not None:
            kw["accum_out"] = accum.ap
        S.op(eng, lambda e: e.tensor_scalar(out=out.ap, in0=in0.ap, scalar1=apof(s1), scalar2=apof(s2),
                                            op0=op0, **kw),
             reads=bufs(in0, s1, s2), writes=bufs(out, accum))

    def tt(eng, out, in0, in1, op):
        S.op(eng, lambda e: e.tensor_tensor(out=out.ap, in0=in0.ap, in1=in1.ap, op=op),
             reads=bufs(in0, in1), writes=bufs(out))

    def stt(eng, out, in0, scalar, in1, op0, op1):
        S.op(eng, lambda e: e.scalar_tensor_tensor(out=out.ap, in0=in0.ap, scalar=apof(scalar), in1=in1.ap,
                                                   op0=op0, op1=op1),
             reads=bufs(in0, scalar, in1), writes=bufs(out))

    def cp(eng, out, in_):
        if eng == "act":
            S.op(eng, lambda e: e.copy(out=out.ap, in_=in_.ap), reads=bufs(in_), writes=bufs(out))
        else:
            S.op(eng, lambda e: e.tensor_copy(out=out.ap, in_=in_.ap), reads=bufs(in_), writes=bufs(out))

    def memset(eng, out, val):
        S.op(eng, lambda e: e.memset(out.ap, val), writes=bufs(out))

    def recip(out, in_):
        S.op("dve", lambda e: e.reciprocal(out=out.ap, in_=in_.ap), reads=bufs(in_), writes=bufs(out))

    def mm(out, lhsT, rhs, start=True, stop=True, last=True):
        bp = lhsT.ap.base_partition()
        kw = {"tile_position": (bp, 0)} if bp else {}
        S.op("pe", lambda e: e.matmul(out.ap, lhsT=lhsT.ap, rhs=rhs.ap, start=start, stop=stop, **kw),
             reads=bufs(lhsT, rhs), writes=bufs(out), pe_acc=True, inc=last)

    def tr(out, in_, ident, last=True):
        S.op("pe", lambda e: e.transpose(out=out.ap, in_=in_.ap, identity=ident.ap),
             reads=bufs(in_, ident), writes=bufs(out), pe_acc=True, inc=last)

    rr = {"ev": 0}

    def evac_eng():
        rr["ev"] ^= 1
        return "act" if rr["ev"] else "dve"

    with es:
        ident_f = sb([128, 128], F32)
        ident_b = sb([128, 128], BF16)
        masks_b = sb([128, 256], BF16)
        for mi_ in range(2):
            dma("sp", ident_f[:], D["masks"][:, mi_ * 128:(mi_ + 1) * 128])
            cp("dve", masks_b[:, mi_ * 128:(mi_ + 1) * 128], ident_f[:])
        dma("sp", ident_f[:], D["ident"])
        cp("dve", ident_b[:], ident_f[:])
        sel_t = sb([128, 2], F32)
        dma("sp", sel_t[:], D["sel"])
        tri64 = sb([128, 640], F32)
        dma("sp", tri64[:], D["tri64"])
        ones_col = sb([128, 1], F32)
        memset("dve", ones_col[:], 1.0)

        def bcast_row(name, n, stack=None):
            t = sb([128, n], F32, stack)
            dma("sp", t[:], D[name].partition_broadcast(128))
            return t

        kvg_bc = bcast_row("kv_norm_g", 128)
        qg_col = sb([128, 2], F32)
        dma("sp", qg_col[:], D["q_norm_g"])
        lng_col = sb([128, 8], F32)
        dma("sp", lng_col[:], D["ln_g"])
        memg_col = sb([128, 8], F32)
        dma("sp", memg_col[:], D["mem_norm_g"])

        W_uk = sb([128, 512], BF16)
        W_uv = sb([128, 512], BF16)
        mv_aug = sb([128, 2, 4, 65], BF16)
        mkT = sb([64, 4, 256], BF16)
        WUA = sb([64, 2, 256], BF16)
        rw_own = sb([128, 16, 256], BF16)
        rw_smp = sb([128, 4, 256], BF16)
        ckvS_b = sb([128, 4, 128], BF16)
        ckvTS = sb([128, 4, 128], BF16)
        kpeTS = sb([32, 4, 128], BF16)
        Hf = [sb([128, 128], F32) for _ in range(2)]
        Hb = [sb([128, 128], BF16) for _ in range(2)]
        for t_ in Hf:
            memset("pool", t_[:], 0.0)
        for t_ in Hb:
            memset("pool", t_[:], 0.0)
        kT = sb([96, 8, 4096], BF16)
        Vaug = sb([128, 32, 8, 65], BF16)
        memset("pool", Vaug[:], 1.0)

        ps_mm = [ps([128, 512], F32) for _ in range(4)]
        ps_tr = [ps([128, 1024], BF16) for _ in range(2)]
        ps_acc = [ps([128, 512], F32) for _ in range(2)]
        cnt = {"mm": 0, "tr": 0}

        def next_mm():
            cnt["mm"] += 1
            return ps_mm[cnt["mm"] % 4]

        def next_tr():
            cnt["tr"] += 1
            return ps_tr[cnt["tr"] % 2]

        phA = contextlib.ExitStack()
        mu_bc = bcast_row("shift_mu", 896, phA)
        w0_bc = bcast_row("w0", 256, phA)
        a0_bc = bcast_row("a0", 256, phA)
        kkp_bc = bcast_row("k_k", 256, phA)
        ka_bc = bcast_row("k_a", 256, phA)
        rk_bc = bcast_row("r_k", 256, phA)
        lg_bc = bcast_row("lnx_g", 256, phA)
        lb_bc = bcast_row("lnx_b", 256, phA)
        W_A = sb([128, 8, 1056], BF16, phA)
        xt = [sb([128, 1024], F32, phA)] * 2
        xn = [sb([128, 1024], BF16, phA)] * 2
        hT = [sb([128, 8, 128], BF16, phA)] * 2
        ss = [sb([128, 4], F32, phA) for _ in range(2)]
        cst = [sb([128, 64], F32, phA)] * 2
        ckv_f = [sb([128, 128], F32, phA)] * 2
        ckv_b = [sb([128, 128], BF16, phA)] * 2
        ckvT = [sb([128, 128], BF16, phA)] * 2
        kpe_f = [sb([128, 32], F32, phA)] * 2
        kpad = [sb([128, 96], BF16, phA)] * 2
        for t in kpad:
            memset("pool", t[:], 0.0)
        rtmp = [sb([128, 64], F32, phA)] * 2
        zt = [sb([128, 896], F32, phA)] * 2

        def load_norm_T(i, src_rows, g_cols=1024.0):
            s = i % 2
            dma("sp", xt[s][:], src_rows)
            act(xn[s][:], xt[s][:], AF.Square, scale=1.0 / 32.0, accum=ss[s][:, 0:1])
            act(ss[s][:, 1:2], ss[s][:, 0:1], AF.Sqrt, bias=1e-6)
            recip(ss[s][:, 2:3], ss[s][:, 1:2])
            ts("dve", xn[s][:], xt[s][:], ss[s][:, 2:3], None, ALU.mult)
            p = next_tr()
            for k in range(8):
                tr(p[:, k * 128:(k + 1) * 128], xn[s][:, k * 128:(k + 1) * 128], ident_b[:], last=(k == 7))
            cp("act", hT[s][:], Vw(p.t[:].rearrange("p (k t) -> p k t", k=8), p.b))
            return s

        def proj(s, W, c0, c1):
            p = next_mm()
            n = c1 - c0
            for k in range(8):
                mm(p[:, 0:n], hT[s][:, k, :], W[:, k, c0:c1], start=(k == 0), stop=(k == 7), last=(k == 7))
            return p

        def rope_pairs(out_f, src, tab, nh):
            pass

        wtmp = contextlib.ExitStack()
        stage_w = [sb([128, 1056], F32, wtmp), sb([128, 1056], F32, wtmp)]
        W_mem = sb([128, 8, 512], BF16, wtmp)
        mk_b = sb([128, 2, 256], BF16, wtmp)
        mkv_f = [sb([128, 512], F32, wtmp) for _ in range(2)]
        w_in_v = D["w_in"].rearrange("(k p) n -> k p n", p=128)
        for k in range(8):
            st = stage_w[k % 2]
            dma("sp", st[:, 0:160], w_in_v[k][:, 256:416])
            dma("sp", st[:, 160:1056], w_in_v[k][:, 928:1824])
            ts("dve" if k % 2 else "pool", W_A[:, k, :], st[:], lng_col[:, k:k + 1], None, ALU.mult)
        w_mem_v = D["w_mem_kv"].rearrange("(k p) n -> k p n", p=128)
        for k in range(8):
            st = stage_w[k % 2]
            dma("sp", st[:, 0:512], w_mem_v[k])
            ts("dve" if k % 2 else "pool", W_mem[:, k, :], st[:, 0:512], memg_col[:, k:k + 1], None, ALU.mult)
        st = stage_w[0]
        dma("sp", st[:, 0:512], D["w_uk"])
        cp("dve", W_uk[:], st[:, 0:512])
        st = stage_w[1]
        dma("sp", st[:, 0:512], D["w_uv"])
        cp("dve", W_uv[:], st[:, 0:512])
        st = stage_w[0]
        dma("sp", st[0:64, 0:256], D["w_up"])
        dma("sp", st[0:64, 256:512], D["a_up"])
        cp("dve", Vw(WUA.t[:].rearrange("p a n -> p (a n)"), WUA.b), st[0:64, 0:512])

        memset("pool", mv_aug[:], 1.0)
        for i in range(2):
            s = load_norm_T(i, D["mem"][i * 128:(i + 1) * 128, :])
            p = proj(s, W_mem, 0, 512)
            cp("act", mkv_f[i][:], p[:, :])
            dma("sp", O["memk"][i * 128:(i + 1) * 128, :], mkv_f[i][:, 0:256])
            dma("sp", O["memv"][i * 128:(i + 1) * 128, :], mkv_f[i][:, 256:512])
            cp("dve", mk_b[:, i, :], mkv_f[i][:, 0:256])
            cp("dve", Vw(mv_aug.t[:, i, :, 0:64], mv_aug.b),
               Vw(mkv_f[i].t[:, 256:512].rearrange("p (h d) -> p h d", h=4), mkv_f[i].b))
            pt_ = next_tr()
            for h in range(4):
                tr(pt_[0:64, h * 128:(h + 1) * 128], mk_b[:, i, h * 64:(h + 1) * 64], ident_b[:], last=(h == 3))
            cp("act", Vw(mkT.t[:, :, i * 128:(i + 1) * 128], mkT.b),
               Vw(pt_.t[0:64, 0:512].rearrange("p (h t) -> p h t", h=4), pt_.b))

        S.barrier()
        wtmp.close()

        phR = contextlib.ExitStack()
        zprev = sb([128, 896], F32, phR)
        zm = zprev
        ta = sb([128, 128], BF16, phR)
        taT = sb([64, 2, 128], BF16, phR)
        f_ = lambda: sb([128, 256], F32, phR)
        b_ = lambda: sb([128, 256], BF16, phR)
        xw, lw, a_t, kk, kp, beta, tmp1, Gs, Ab_f = [f_() for _ in range(9)]
        e2, tmp2, e1, Y_f, rw_o = xw, Gs, a_t, kk, beta
        Ab, Rb, Bt, Kt, Bh, Kh, v_b = [b_() for _ in range(7)]
        U_c = [b_() for _ in range(4)]
        v_c = [b_() for _ in range(4)]
        for t_ in U_c:
            memset("pool", t_[:], 0.0)
        red = sb([128, 16], F32, phR)
        T4h = sb([64, 4, 4, 128], BF16, phR)
        RT = [sb([128, 128], BF16, phR) for _ in range(2)]
        Lm = [[sb([128, 128], BF16, phR) for _ in range(2)] for _ in range(4)]
        LT = [[sb([128, 128], BF16, phR) for _ in range(2)] for _ in range(4)]
        LA = [sb([128, 256], BF16, phR) for _ in range(4)]
        KA = [sb([128, 256], BF16, phR) for _ in range(4)]
        X_f = [sb([128, 128], F32, phR) for _ in range(4)]
        X_b = [sb([128, 128], BF16, phR) for _ in range(4)]
        WT = [sb([128, 128], BF16, phR) for _ in range(2)]
        gam = sb([128, 4], F32, phR)

        def rwkv_prep(z, tri, nlev):
            tt("dve", zm[:], zprev[:], z[:], ALU.subtract)
            tt("pool", zm[:], zm[:], mu_bc[:], ALU.mult)
            tt("dve", zm[:], zm[:], z[:], ALU.add)
            r_ = zm[:, 0:256]
            k_ = zm[:, 256:512]
            v_ = zm[:, 512:768]
            if KP < 2:
                return
            act(ta[:, 0:64], zm[:, 768:832], AF.Tanh)
            cp("pool", ta[:, 64:128], zm[:, 832:896])
            pt_ = next_tr()
            tr(pt_[0:64, 0:128], ta[:, 0:64], ident_b[:], last=False)
            tr(pt_[0:64, 128:256], ta[:, 64:128], ident_b[:])
            cp("act", taT[:], Vw(pt_.t[0:64, 0:256].rearrange("p (a t) -> p a t", a=2), pt_.b))
            if KP < 3:
                return
            p1 = next_mm()
            mm(p1[:, 0:256], taT[:, 0, :], WUA[:, 0, :])
            mm(p1[:, 256:512], taT[:, 1, :], WUA[:, 1, :])
            tt("dve", xw[:], p1[:, 0:256], w0_bc[:], ALU.add)
            tt("dve", a_t[:], p1[:, 256:512], a0_bc[:], ALU.add)
            if KP < 4:
                return
            act(xw[:], xw[:], AF.Exp, scale=-1.0)
            act(xw[:], xw[:], AF.Ln, bias=1.0)
            act(xw[:], xw[:], AF.Exp, scale=-1.0, bias=-0.5)
            ts("pool", lw[:], xw[:], -1.0, None, ALU.mult)
            act(a_t[:], a_t[:], AF.Sigmoid)
            if KSUB < 2:
                return
            tt("dve", kk[:], k_, kkp_bc[:], ALU.mult)
            tt("pool", tmp1[:], kk[:], kk[:], ALU.mult)
            S.op("dve", lambda e: e.reduce_sum(out=red.t[:, 0:4], in_=tmp1.t[:].rearrange("p (h k) -> p h k", h=4), axis=AX.X),
                 reads=[tmp1.b], writes=[red.b])
            act(red[:, 0:4], red[:, 0:4], AF.Sqrt)
            ts("dve", red[:, 0:4], red[:, 0:4], 1e-12, None, ALU.max)
            recip(red[:, 0:4], red[:, 0:4])
            for h in range(4):
                ts("dve" if h % 2 else "pool", kk[:, h * 64:(h + 1) * 64], kk[:, h * 64:(h + 1) * 64], red[:, h:h + 1], None, ALU.mult)
            stt("dve", kp[:], a_t[:], -1.0, ka_bc[:], ALU.add, ALU.mult)
            stt("dve", kp[:], kp[:], 1.0, k_, ALU.add, ALU.mult)
            tt("pool", beta[:], kk[:], a_t[:], ALU.mult)
            tt("pool", tmp1[:], r_, kp[:], ALU.mult)
            tt("dve", tmp1[:], tmp1[:], rk_bc[:], ALU.mult)
            S.op("dve", lambda e: e.reduce_sum(out=red.t[:, 4:8], in_=tmp1.t[:].rearrange("p (h k) -> p h k", h=4), axis=AX.X),
                 reads=[tmp1.b], writes=[red.b])
            cp("act", v_b[:], v_)
            if KSUB < 3:
                return
            pg = next_mm()
            mm(pg[:, 0:256], tri[:, 0:128], lw[:])
            mm(pg[:, 256:512], tri[:, 128:256], lw[:])
            cp("act", Gs[:], pg[:, 0:256])
            if KQ < 2:
                return
            tt("dve", e1[:], Gs[:], lw[:], ALU.subtract)
            act(e1[:], e1[:], AF.Exp)
            stt("dve", Ab_f[:], kk[:], -1.0, e1[:], ALU.mult, ALU.mult)
            cp("pool", Ab[:], Ab_f[:])
            if KQ < 3:
                return
            act(e2[:], Gs[:], AF.Exp)
            tt("dve", Rb[:], r_, e2[:], ALU.mult)
            act(e2[:], Gs[:], AF.Exp, scale=-1.0)
            tt("dve", Bt[:], beta[:], e2[:], ALU.mult)
            tt("pool", Kt[:], kp[:], e2[:], ALU.mult)
            if KQ < 4:
                return
            tt("dve", e1[:], pg[:, 256:512], Gs[:], ALU.subtract)
            act(e1[:], e1[:], AF.Exp)
            tt("dve", Bh[:], beta[:], e1[:], ALU.mult)
            tt("pool", Kh[:], kp[:], e1[:], ALU.mult)
            if KSUB < 4:
                return
            if KR < 1:
                return
            for h in range(4 if (KT & 1) else 0):
                pt_ = next_tr()
                for qi, src_ in enumerate((Ab, Rb, Bt, Kt)):
                    tr(pt_[0:64, qi * 128:(qi + 1) * 128], src_[:, h * 64:(h + 1) * 64], ident_b[:], last=(qi == 3))
                cp(evac_eng(), T4h[:, h, :, :], Vw(pt_.t[0:64, 0:512].rearrange("p (q t) -> p q t", q=4), pt_.b))
            pt_ = next_tr()
            for hp in range(2 if (KT & 2) else 0):
                tr(pt_[:, hp * 128:(hp + 1) * 128], Rb[:, hp * 128:(hp + 1) * 128], ident_b[:], last=(hp == 1))
            for hp in range(2 if (KT & 2) else 0):
                cp(evac_eng(), RT[hp][:], pt_[:, hp * 128:(hp + 1) * 128])
            if KR < 2:
                return
            for h in range(4):
                AbT = T4h[:, h, 0, :]
                ARt = Vw(T4h.t[:, h, 0:2, :].rearrange("p q t -> p (q t)"), T4h.b)
                BtT = T4h[:, h, 2, :]
                KtT = T4h[:, h, 3, :]
                p = next_mm()
                mm(p[:, 0:128], AbT, BtT)
                tt("dve", Lm[h][0][:], p[:, 0:128], tri[:, 512:640], ALU.mult)
                p = next_mm()
                mm(p[:, 0:256], BtT, ARt)
                tt("dve", LA[h][:], p[:, 0:256], tri[:, 256:512], ALU.mult)
                p = next_mm()
                mm(p[:, 0:256], KtT, ARt)
                tt("dve", KA[h][:], p[:, 0:256], tri[:, 256:512], ALU.mult)
                if KR < 3:
                    continue
                p = next_mm()
                mm(p[:, 0:64], KA[h][:, 0:128], v_b[:, h * 64:(h + 1) * 64])
                cp("act", X_f[h][:, 64:128], p[:, 0:64])
                cp("pool", X_f[h][:, 0:64], Ab_f[:, h * 64:(h + 1) * 64])
                cp("pool", X_b[h][:], X_f[h][:])
            for lev in range(nlev if KR >= 4 else 0):
                cur = lev % 2
                for h in range(4):
                    L_cur = Lm[h][cur][:]
                    LT_cur = LA[h][:, 0:128] if lev == 0 else LT[h][cur][:]
                    p = next_mm()
                    mm(p[:, 0:128], LT_cur, X_b[h][:])
                    tt("dve", X_f[h][:], X_f[h][:], p[:, 0:128], ALU.add)
                    cp("pool", X_b[h][:], X_f[h][:])
                    if lev < nlev - 1:
                        p2 = next_mm()
                        mm(p2[:, 0:128], LT_cur, L_cur)
                        mm(p2[:, 128:256], L_cur, LT_cur)
                        cp("act", Lm[h][1 - cur][:], p2[:, 0:128])
                        cp("dve", LT[h][1 - cur][:], p2[:, 128:256])
            for hp in range(2 if KR >= 5 else 0):
                pt_ = next_tr()
                for hh in range(2):
                    h = hp * 2 + hh
                    cp("pool", ta[:, hh * 64:(hh + 1) * 64], X_f[h][:, 0:64])
                tr(pt_[:, 0:128], ta[:], ident_b[:])
                cp("act", WT[hp][:], pt_[:, 0:128])

        def rwkv_mask_v(slot, base, tri):
            ts("pool", v_c[slot][:], v_b[:], tri[:, 128 + base:129 + base], None, ALU.mult)

        def rwkv_seq(base, n, slot, hp, Hf_, Hb_, tri):
            rs = slice(base, base + n)
            cs = slice(hp * 128, (hp + 1) * 128)
            Uc, vc = U_c[slot], v_c[slot]
            ind = tri[:, 128 + base:129 + base]
            pgm = next_mm()
            mm(pgm[:, 0:1], Vw(lw.t[:, cs], lw.b), ind)
            act(gam[:, hp:hp + 1], pgm[:, 0:1], AF.Exp)
            pu = next_mm()
            mm(pu[:, 0:128], WT[hp][:], Hb_[:])
            for hh in range(2):
                h = hp * 2 + hh
                tt("dve", Vw(Uc.t[rs, h * 64:(h + 1) * 64], Uc.b), Vw(pu.t[rs, hh * 64:(hh + 1) * 64], pu.b),
                   Vw(X_f[h].t[rs, 64:128], X_f[h].b), ALU.add)
            py = next_mm()
            mm(py[:, 0:128], RT[hp][:], Hb_[:], start=True, stop=False, last=False)
            for hh in range(2):
                h = hp * 2 + hh
                mm(py[:, hh * 64:(hh + 1) * 64], LA[h][:, 128:256], Uc[:, h * 64:(h + 1) * 64],
                   start=False, stop=False, last=False)
                mm(py[:, hh * 64:(hh + 1) * 64], KA[h][:, 128:256], vc[:, h * 64:(h + 1) * 64],
                   start=False, stop=(hh == 1), last=(hh == 1))
            cp("act", Vw(Y_f.t[rs, cs], Y_f.b), Vw(py.t[rs, 0:128], py.b))
            ph_ = next_mm()
            mm(ph_[:, 0:128], Bh[:, cs], Uc[:, cs], start=True, stop=False, last=False)
            mm(ph_[:, 0:128], Kh[:, cs], vc[:, cs], start=False, stop=True, last=True)
            for hh in range(2):
                blk = slice(hh * 64, (hh + 1) * 64)
                stt("dve", Vw(Hf_.t[blk, blk], Hf_.b), Vw(Hf_.t[blk, blk], Hf_.b), Vw(gam.t[blk, hp:hp + 1], gam.b),
                    Vw(ph_.t[blk, blk], ph_.b), ALU.mult, ALU.add)
            cp("act", Hb_[:], Hf_[:])

        def rwkv_post(z):
            v_ = zm[:, 512:768]
            S.op("dve", lambda e: e.reduce_sum(out=red.t[:, 8:12], in_=Y_f.t[:].rearrange("p (h k) -> p h k", h=4), axis=AX.X),
                 reads=[Y_f.b], writes=[red.b])
            ts("dve", red[:, 8:12], red[:, 8:12], -1.0 / 64, None, ALU.mult)
            for h in range(4):
                ts("dve" if h % 2 else "pool", tmp2[:, h * 64:(h + 1) * 64], Y_f[:, h * 64:(h + 1) * 64], red[:, 8 + h:9 + h], None, ALU.add)
            tt("pool", tmp1[:], tmp2[:], tmp2[:], ALU.mult)
            S.op("dve", lambda e: e.reduce_sum(out=red.t[:, 12:16], in_=tmp1.t[:].rearrange("p (h k) -> p h k", h=4), axis=AX.X),
                 reads=[tmp1.b], writes=[red.b])
            act(red[:, 12:16], red[:, 12:16], AF.Sqrt, scale=1.0 / 64, bias=64e-5)
            recip(red[:, 12:16], red[:, 12:16])
            for h in range(4):
                ts("dve" if h % 2 else "pool", tmp2[:, h * 64:(h + 1) * 64], tmp2[:, h * 64:(h + 1) * 64], red[:, 12 + h:13 + h], None, ALU.mult)
            tt("dve", tmp2[:], tmp2[:], lg_bc[:], ALU.mult)
            tt("pool", tmp2[:], tmp2[:], lb_bc[:], ALU.add)
            for h in range(4):
                stt("dve", rw_o[:, h * 64:(h + 1) * 64], Vw(zm.t[:, 512 + h * 64:512 + (h + 1) * 64], zm.b), red[:, 4 + h:5 + h],
                    tmp2[:, h * 64:(h + 1) * 64], ALU.mult, ALU.add)

        NBLK = int(os.environ.get("KNBLK", "32")) if STAGE >= 1 else 0
        for i in range(NBLK):
            s = load_norm_T(i, D["x_seq"][i * 128:(i + 1) * 128, :])
            dma("sp", cst[s][:], D["cs_seq"][i * 128:(i + 1) * 128, :])
            p = proj(s, W_A, 0, 160)
            act(ckv_b[s][:], p[:, 0:128], AF.Square, scale=128 ** -0.5, accum=ss[s][:, 3:4])
            act(ss[s][:, 3:4], ss[s][:, 3:4], AF.Sqrt, bias=1e-6)
            recip(ss[s][:, 3:4], ss[s][:, 3:4])
            stt("dve", ckv_f[s][:], p[:, 0:128], ss[s][:, 3:4], kvg_bc[:], ALU.mult, ALU.mult)
            dma("sp", O["ckv_seq"][i * 128:(i + 1) * 128, :], ckv_f[s][:])
            cp("pool", ckv_b[s][:], ckv_f[s][:])
            tt("dve", rtmp[s][:, 0:32], p[:, 128:160], cst[s][:, 0:32], ALU.mult)
            tt("dve", rtmp[s][:, 32:64], p[:, 128:160], cst[s][:, 32:64], ALU.mult)
            tt("dve", kpe_f[s][:, 0:16], rtmp[s][:, 0:16], rtmp[s][:, 48:64], ALU.subtract)
            tt("dve", kpe_f[s][:, 16:32], rtmp[s][:, 16:32], rtmp[s][:, 32:48], ALU.add)
            dma("sp", O["kpe_seq"][i * 128:(i + 1) * 128, :], kpe_f[s][:])
            cp("pool", kpad[s][:, 64:96], kpe_f[s][:])
            pt_ = next_tr()
            tr(pt_[:, 0:128], ckv_b[s][:], ident_b[:], last=False)
            tr(pt_[0:96, 128:256], kpad[s][:], ident_b[:], last=True)
            cp("act", ckvT[s][:], pt_[:, 0:128])
            for h in range(8):
                cp("dve" if h % 2 else "act", kT[64:96, h, i * 128:(i + 1) * 128], pt_[64:96, 128:256])
            p = next_mm()
            mm(p[:, :], ckvT[s][:], W_uv[:], last=True)
            cp("dve", Vw(Vaug.t[:, i, :, 0:64], Vaug.b), Vw(p.t[:, :].rearrange("p (h d) -> p h d", h=8), p.b))
            for hh in range(2):
                p = next_mm()
                for h4 in range(4):
                    h = hh * 4 + h4
                    mm(p[0:64, h4 * 128:(h4 + 1) * 128], W_uk[:, h * 64:(h + 1) * 64], ckvT[s][:], last=(h4 == 3))
                cp("act", Vw(kT.t[0:64, hh * 4:(hh + 1) * 4, i * 128:(i + 1) * 128], kT.b),
                   Vw(p.t[0:64, :].rearrange("p (h t) -> p h t", h=4), p.b))
            if STAGE >= 2:
                if i == 0:
                    memset("pool", zprev[0:1, :], 0.0)
                else:
                    dma("sp", zprev[0:1, :], zt[s][127:128, :])
            for half in range(2):
                p = proj(s, W_A, 160 + half * 448, 160 + (half + 1) * 448)
                cp(evac_eng(), zt[s][:, half * 448:(half + 1) * 448], p[:, 0:448])
            if i == NBLK - 1:
                dma("sp", O["shift_p"], zt[s][127:128, :])
            if STAGE >= 2:
                dma("sp", zprev[1:128, :], zt[s][0:127, :])
                if KSUB >= 1:
                    rwkv_prep(zt[s], tri64, 6)
                for c in range(2):
                    if KSUB >= 5:
                        rwkv_mask_v(c, c * 64, tri64)
                    for hp in range(2):
                        if KSUB >= 5:
                            rwkv_seq(c * 64, 64, c, hp, Hf[hp], Hb[hp], tri64)
                if KSUB >= 6:
                    rwkv_post(zt[s])
                if i % 2 == 0:
                    ts("dve", rw_own[:, i // 2, :], rw_o[:], sel_t[:, 0:1], None, ALU.mult)
                else:
                    stt("dve", rw_own[:, i // 2, :], rw_o[:], sel_t[:, 1:2], rw_own[:, i // 2, :], ALU.mult, ALU.add)
        stS = sb([128, 128], F32, phR)
        if STAGE >= 2 and os.environ.get("KFIN", "1") == "1":
            for hp in range(2):
                ptf = next_mm()
                tr(ptf[:, 0:128], Hf[hp][:], ident_f[:])
                cp("act", stS[:], ptf[:, 0:128])
                for hh in range(2):
                    blk = slice(hh * 64, (hh + 1) * 64)
                    dma("sp", O["wkv_p"][hp * 2 + hh], Vw(stS.t[blk, blk], stS.b))

        if STAGE >= 4:
            tri8 = sb([128, 640], F32, phR)
            dma("sp", tri8[:], D["tri8"])
            Hs_f = sb([128, 128], F32, phR)
            Hs_b = sb([128, 128], BF16, phR)
            memset("pool", Hs_f[:], 0.0)
            Sin = sb([64, 128], F32, phR)
            stS2 = stS
            kpe_b = sb([128, 32], BF16, phR)
            for t_ in U_c:
                memset("pool", t_[:], 0.0)
            for tau in range(4):
                i = 32 + tau
                s = load_norm_T(i, D["x_smp"][tau * 128:(tau + 1) * 128, :])
                dma("sp", cst[s][:], D["cs_smp"][tau * 128:(tau + 1) * 128, :])
                p = proj(s, W_A, 0, 160)
                act(ckv_b[s][:], p[:, 0:128], AF.Square, scale=128 ** -0.5, accum=ss[s][:, 3:4])
                act(ss[s][:, 3:4], ss[s][:, 3:4], AF.Sqrt, bias=1e-6)
                recip(ss[s][:, 3:4], ss[s][:, 3:4])
                stt("dve", ckv_f[s][:], p[:, 0:128], ss[s][:, 3:4], kvg_bc[:], ALU.mult, ALU.mult)
                cp("pool", ckvS_b[:, tau, :], ckv_f[s][:])
                tt("dve", rtmp[s][:, 0:32], p[:, 128:160], cst[s][:, 0:32], ALU.mult)
                tt("dve", rtmp[s][:, 32:64], p[:, 128:160], cst[s][:, 32:64], ALU.mult)
                tt("dve", kpe_f[s][:, 0:16], rtmp[s][:, 0:16], rtmp[s][:, 48:64], ALU.subtract)
                tt("dve", kpe_f[s][:, 16:32], rtmp[s][:, 16:32], rtmp[s][:, 32:48], ALU.add)
                cp("pool", kpe_b[:], kpe_f[s][:])
                for q in range(4):
                    sq = 4 * tau + q
                    dma("sp", O["ckv_s"][sq * 8:(sq + 1) * 8, :], ckv_f[s][32 * q:32 * q + 8, :])
                    dma("sp", O["kpe_s"][sq * 8:(sq + 1) * 8, :], kpe_f[s][32 * q:32 * q + 8, :])
                pt_ = next_tr()
                tr(pt_[:, 0:128], ckvS_b[:, tau, :], ident_b[:], last=False)
                tr(pt_[0:32, 128:256], kpe_b[:], ident_b[:], last=True)
                cp("act", ckvTS[:, tau, :], pt_[:, 0:128])
                cp("dve", kpeTS[:, tau, :], pt_[0:32, 128:256])
                for half in range(2):
                    p = proj(s, W_A, 160 + half * 448, 160 + (half + 1) * 448)
                    cp(evac_eng(), zt[s][:, half * 448:(half + 1) * 448], p[:, 0:448])
                dma("sp", zprev[1:128, :], zt[s][0:127, :])
                for q in range(4):
                    sq = 4 * tau + q
                    dma("sp", O["shift_s"][sq:sq + 1, :], zt[s][32 * q + 7:32 * q + 8, :])
                    dma("sp", zprev[32 * q:32 * q + 1, :], D["state_shift"][sq:sq + 1, :])
                rwkv_prep(zt[s], tri8, 3)
                for q in range(4):
                    sq = 4 * tau + q
                    rwkv_mask_v(q, 32 * q, tri8)
                    for hp in range(2):
                        dma("sp", Vw(Sin.t[:].rearrange("v (h k) -> v h k", h=2), Sin.b),
                            D["state_wkv"][sq, 2 * hp:2 * hp + 2].rearrange("h v k -> v h k"))
                        ptf = next_mm()
                        tr(ptf[:, 0:64], Sin[:], ident_f[0:64, 0:64])
                        cp("act", Hs_f[0:64, 0:64], ptf[0:64, 0:64])
                        cp("dve", Hs_f[64:128, 64:128], ptf[64:128, 0:64])
                        cp("act", Hs_b[:], Hs_f[:])
                        rwkv_seq(32 * q, 8, q, hp, Hs_f, Hs_b, tri8)
                        ptf = next_mm()
                        tr(ptf[:, 0:128], Hs_f[:], ident_f[:])
                        cp("act", stS2[:], ptf[:, 0:128])
                        for hh in range(2):
                            blk = slice(hh * 64, (hh + 1) * 64)
                            dma("sp", O["wkv_s"][sq, hp * 2 + hh], Vw(stS2.t[blk, blk], stS2.b))
                rwkv_post(zt[s])
                cp("pool", rw_smp[:, tau, :], rw_o[:])

        S.barrier()
        phR.close()
        phA.close()

        if STAGE >= 3:
            phB = contextlib.ExitStack()
            GQ = 2
            NG = 16 // GQ
            NQ = GQ * 128
            W_B = sb([128, 8, 1536], BF16, phB)
            W_uq = sb([128, 2, 768], BF16, phB)
            W_out = sb([128, 8, 1024], BF16, phB)
            fing_bc = bcast_row("final_g", 1024, phB)
            tmpB = contextlib.ExitStack()
            stB = [sb([128, 1536], F32, tmpB) for _ in range(2)]
            for k in range(8):
                st = stB[k % 2]
                dma("sp", st[:, 0:256], w_in_v[k][:, 0:256])
                dma("sp", st[:, 256:768], w_in_v[k][:, 416:928])
                dma("sp", st[:, 768:1536], w_in_v[k][:, 1824:2592])
                ts("dve" if k % 2 else "pool", W_B[:, k, :], st[:], lng_col[:, k:k + 1], None, ALU.mult)
            w_uq_v = D["w_uq"].rearrange("(k p) n -> k p n", p=128)
            for k in range(2):
                st = stB[k % 2]
                dma("sp", st[:, 0:768], w_uq_v[k])
                ts("dve", W_uq[:, k, :], st[:, 0:768], qg_col[:, k:k + 1], None, ALU.mult)
            w_out_v = D["w_out"].rearrange("(k p) n -> k p n", p=128)
            for k in range(8):
                st = stB[k % 2]
                dma("sp", st[:, 0:1024], w_out_v[k])
                cp("dve" if k % 2 else "pool", W_out[:, k, :], st[:, 0:1024])
            S.barrier()
            tmpB.close()
            phB2 = contextlib.ExitStack()
            xtB = sb([128, 1024], F32, phB2)
            xnB = sb([128, 1024], BF16, phB2)
            hTB = sb([128, 8, 128], BF16, phB2)
            ssB = sb([128, 8], F32, phB2)
            cstB = sb([128, 64], F32, phB2)
            cqn = sb([128, 256], BF16, phB2)
            cqT = sb([128, 2, 128], BF16, phB2)
            q_f = sb([128, 8, 96], F32, phB2)
            q_b = sb([128, 8, 96], BF16, phB2)
            rq = sb([128, 8, 64], F32, phB2)
            qT = sb([96, 8, NQ], BF16, phB2)
            qx_b = sb([128, 256], BF16, phB2)
            qxT = sb([64, 4, NQ], BF16, phB2)
            gate = sb([128, GQ, 1024], BF16, phB2)
            mixed = sb([128, GQ, 1024], BF16, phB2)
            Pt = [sb([128, NQ], BF16, phB2) for _ in range(2)]
            OT = sb([65, NQ], F32, phB2)
            rden = sb([128, GQ], F32, phB2)
            mT = sb([128, 8, 128], BF16, phB2)

            def loadB(src_rows):
                dma("sp", xtB[:], src_rows)
                act(xnB[:], xtB[:], AF.Square, scale=1.0 / 32.0, accum=ssB[:, 0:1])
                act(ssB[:, 1:2], ssB[:, 0:1], AF.Sqrt, bias=1e-6)
                recip(ssB[:, 2:3], ssB[:, 1:2])
                ts("dve", xnB[:], xtB[:], ssB[:, 2:3], None, ALU.mult)
                p = next_tr()
                for k in range(8):
                    tr(p[:, k * 128:(k + 1) * 128], xnB[:, k * 128:(k + 1) * 128], ident_b[:], last=(k == 7))
                cp("act", hTB[:], Vw(p.t[:].rearrange("p (k t) -> p k t", k=8), p.b))

            def projB(c0, c1):
                p = next_mm()
                n = c1 - c0
                for k in range(8):
                    mm(p[:, 0:n], hTB[:, k, :], W_B[:, k, c0:c1], start=(k == 0), stop=(k == 7), last=(k == 7))
                return p

            def q_tile(x_rows, cs_rows, rw_src, qb):
                loadB(x_rows)
                dma("sp", cstB[:], cs_rows)
                p = projB(0, 256)
                act(cqn[:], p[:, 0:256], AF.Square, scale=1.0 / 16.0, accum=ssB[:, 3:4])
                act(ssB[:, 3:4], ssB[:, 3:4], AF.Sqrt, bias=1e-6)
                recip(ssB[:, 3:4], ssB[:, 3:4])
                ts("dve", cqn[:], p[:, 0:256], ssB[:, 3:4], None, ALU.mult)
                pt_ = next_tr()
                for k in range(2):
                    tr(pt_[:, k * 128:(k + 1) * 128], cqn[:, k * 128:(k + 1) * 128], ident_b[:], last=(k == 1))
                cp("act", cqT[:], Vw(pt_.t[:, 0:256].rearrange("p (k t) -> p k t", k=2), pt_.b))
                for half in range(2):
                    p = next_mm()
                    for k in range(2):
                        mm(p[:, 0:384], cqT[:, k, :], W_uq[:, k, half * 384:(half + 1) * 384], start=(k == 0), stop=(k == 1), last=(k == 1))
                    cp(evac_eng(), Vw(q_f.t[:, half * 4:(half + 1) * 4, :], q_f.b),
                       Vw(p.t[:, 0:384].rearrange("p (h d) -> p h d", h=4), p.b))
                for h in range(8):
                    e_ = "dve" if h % 2 else "pool"
                    tt(e_, rq[:, h, 0:32], q_f[:, h, 64:96], cstB[:, 0:32], ALU.mult)
                    tt(e_, rq[:, h, 32:64], q_f[:, h, 64:96], cstB[:, 32:64], ALU.mult)
                    tt(e_, q_f[:, h, 64:80], rq[:, h, 0:16], rq[:, h, 48:64], ALU.subtract)
                    tt(e_, q_f[:, h, 80:96], rq[:, h, 16:32], rq[:, h, 32:48], ALU.add)
                cp("act", q_b[:], q_f[:])
                for hh in range(2):
                    pt_ = next_tr()
                    for h4 in range(4):
                        tr(pt_[0:96, h4 * 128:(h4 + 1) * 128], q_b[:, hh * 4 + h4, :], ident_b[:], last=(h4 == 3))
                    cp(evac_eng(), Vw(qT.t[:, hh * 4:(hh + 1) * 4, qb * 128:(qb + 1) * 128], qT.b),
                       Vw(pt_.t[0:96, 0:512].rearrange("p (h t) -> p h t", h=4), pt_.b))
                p = projB(256, 768)
                act(gate[:, qb, 0:512], p[:, 0:512], AF.Silu)
                p = projB(768, 1280)
                act(gate[:, qb, 512:768], p[:, 0:256], AF.Silu)
                cp("dve", qx_b[:], p[:, 256:512])
                tt("dve", mixed[:, qb, 512:768], gate[:, qb, 512:768], rw_src, ALU.mult)
                p = projB(1280, 1536)
                act(gate[:, qb, 768:1024], p[:, 0:256], AF.Silu)
                pt_ = next_tr()
                for h in range(4):
                    tr(pt_[0:64, h * 128:(h + 1) * 128], qx_b[:, h * 64:(h + 1) * 64], ident_b[:], last=(h == 3))
                cp(evac_eng(), Vw(qxT.t[:, :, qb * 128:(qb + 1) * 128], qxT.b),
                   Vw(pt_.t[0:64, 0:512].rearrange("p (h t) -> p h t", h=4), pt_.b))

            acc_i = [0]

            def attn_head(kv_list, kT_of, V_of, q_rhs, scale, mask_of, out_col):
                acc_i[0] ^= 1
                acc = ps_acc[acc_i[0]]
                nkv = len(kv_list)
                for n_, (j, c0) in enumerate(kv_list):
                    p = next_mm()
                    ml = mask_of(j)
                    mm(p[:, c0:NQ], kT_of(j), Vw(q_rhs.ap[:, c0:NQ], q_rhs.b), start=True, stop=(len(ml) == 0), last=(len(ml) == 0))
                    for mi, (qb_, mcol) in enumerate(ml):
                        mm(p[:, qb_ * 128:(qb_ + 1) * 128], ident_b[:], masks_b[:, mcol * 128:(mcol + 1) * 128],
                           start=False, stop=(mi == len(ml) - 1), last=(mi == len(ml) - 1))
                    pt_t = Pt[n_ % 2]
                    act(pt_t[:, c0:NQ], p[:, c0:NQ], AF.Exp, scale=scale)
                    mm(acc[0:65, c0:NQ], V_of(j), pt_t[:, c0:NQ], start=(n_ == 0), stop=(n_ == nkv - 1), last=True)
                cp("act", OT[:], acc[0:65, 0:NQ])
                po = next_mm()
                for qb_ in range(GQ):
                    tr(po[:, qb_ * 65:(qb_ + 1) * 65], OT[:, qb_ * 128:(qb_ + 1) * 128], ident_f[0:65, 0:65], last=(qb_ == GQ - 1))
                S.op("dve", lambda e: e.reciprocal(out=rden.t[:, 0:GQ], in_=po.t[:, 0:GQ * 65].rearrange("p (q d) -> p q d", q=GQ)[:, :, 64]),
                     reads=[po.b], writes=[rden.b])
                for qb_ in range(GQ):
                    stt("dve", mixed[:, qb_, out_col:out_col + 64], po[:, qb_ * 65:qb_ * 65 + 64], rden[:, qb_:qb_ + 1],
                        gate[:, qb_, out_col:out_col + 64], ALU.mult, ALU.mult)

            def out_tile(x_rows, qb, y_rows):
                pt_ = next_tr()
                for k in range(8):
                    tr(pt_[:, k * 128:(k + 1) * 128], mixed[:, qb, k * 128:(k + 1) * 128], ident_b[:], last=(k == 7))
                cp("act", mT[:], Vw(pt_.t[:].rearrange("p (k t) -> p k t", k=8), pt_.b))
                dma("sp", xtB[:], x_rows)
                for half in range(2):
                    p = next_mm()
                    for k in range(8):
                        mm(p[:, :], mT[:, k, :], W_out[:, k, half * 512:(half + 1) * 512], start=(k == 0), stop=(k == 7), last=(k == 7))
                    tt("dve", xtB[:, half * 512:(half + 1) * 512], xtB[:, half * 512:(half + 1) * 512], p[:, :], ALU.add)
                act(xnB[:], xtB[:], AF.Square, scale=1.0 / 32.0, accum=ssB[:, 4:5])
                act(ssB[:, 4:5], ssB[:, 4:5], AF.Sqrt, bias=1e-6)
                recip(ssB[:, 4:5], ssB[:, 4:5])
                stt("dve", xtB[:], xtB[:], ssB[:, 4:5], fing_bc[:], ALU.mult, ALU.mult)
                if isinstance(y_rows, list):
                    for (dst, r0, r1) in y_rows:
                        dma("sp", dst, xtB[r0:r1, :])
                else:
                    dma("sp", y_rows, xtB[:])

            for g in range(NG if STAGE >= 3 else 0):
                for qb in range(GQ):
                    m = g * GQ + qb
                    q_tile(D["x_own"][m * 128:(m + 1) * 128, :], D["cs_own"][m * 128:(m + 1) * 128, :], rw_own[:, m, :], qb)
                jmax = 2 * (g * GQ + GQ - 1) + 1
                kv_list = []
                for j in range(jmax + 1):
                    qb0 = max(0, -(-(j - 1 - 2 * g * GQ) // 2))
                    kv_list.append((j, qb0 * 128))

                def mask_of(j, g=g):
                    ml = []
                    for qb_ in range(GQ):
                        m_ = g * GQ + qb_
                        if j == 2 * m_:
                            ml.append((qb_, 0))
                        elif j == 2 * m_ + 1:
                            ml.append((qb_, 1))
                    return ml

                for h in range(8):
                    attn_head(kv_list, lambda j, h=h: kT[:, h, j * 128:(j + 1) * 128], lambda j, h=h: Vaug[:, j, h, :],
                              qT[:, h, :], MLA_SCALE, mask_of, h * 64)
                for h in range(4):
                    attn_head([(0, 0), (1, 0)], lambda j, h=h: mkT[:, h, j * 128:(j + 1) * 128], lambda j, h=h: mv_aug[:, j, h, :],
                              qxT[:, h, :], X_SCALE, lambda j: [], 768 + h * 64)
                for qb in range(GQ):
                    m = g * GQ + qb
                    out_tile(D["x_own"][m * 128:(m + 1) * 128, :], qb, O["y_own"][m * 128:(m + 1) * 128, :])

            if STAGE >= 5:
                phS = contextlib.ExitStack()
                S.barrier()
                pools = {128: [Vaug.t[:].rearrange("p a b c -> p (a b c)"), 0, 32 * 8 * 65],
                         96: [kT.t[:].rearrange("p a b -> p (a b)"), 0, 8 * 4096]}

                def carve(parts, shape, dt):
                    key = 128 if parts > 96 else 96
                    flat, off, cap = pools[key]
                    n = int(np.prod(shape)) * (1 if dt == BF16 else 2)
                    n += n % 2
                    assert off + n <= cap, (parts, shape, off, n, cap)
                    ap = flat[0:parts, off:off + n]
                    pools[key][1] = off + n
                    if dt != BF16:
                        ap = ap.bitcast(dt)
                    ap = ap[:, 0:int(np.prod(shape))]
                    if len(shape) == 2:
                        ap = ap.rearrange("p (a b) -> p a b", a=shape[0])
                    elif len(shape) == 3:
                        ap = ap.rearrange("p (a b c) -> p a b c", a=shape[0], b=shape[1])
                    return Tl(ap)

                WukT = carve(64, [8, 128], BF16)
                for hh in range(2):
                    pt_ = next_tr()
                    for h4 in range(4):
                        h = hh * 4 + h4
                        tr(pt_[0:64, h4 * 128:(h4 + 1) * 128], W_uk[:, h * 64:(h + 1) * 64], ident_b[:], last=(h4 == 3))
                    cp(evac_eng(), Vw(WukT.t[:, hh * 4:(hh + 1) * 4, :], WukT.b),
                       Vw(pt_.t[0:64, 0:512].rearrange("p (h t) -> p h t", h=4), pt_.b))
                ones_b = carve(128, [128], BF16)
                memset("pool", ones_b[:], 1.0)
                mk_sf = sb([128, 256], F32, phS)
                msk_b = carve(128, [4, 64], BF16)
                dma("sp", mk_sf[:], D["masks_s"])
                cp("dve", Vw(msk_b.t[:].rearrange("p q n -> p (q n)"), msk_b.b), mk_sf[:])
                ptb = sb([128, 64], I32, phS)
                ptf = sb([128, 64], F32, phS)
                idx_t = [sb([128, 64], I32, phS) for _ in range(2)]
                io_t = sb([128, 1], F32, phS)
                dma("sp", io_t[:], D["iota_p"])
                pt_rows = D["page_table"].rearrange("o (s p) -> (o s) p", p=64)
                qpeT = carve(32, [8, 128], BF16)
                qlat_all = carve(128, [8, 128], BF16)
                qlat_s = carve(128, [64], BF16)
                qpe_s = carve(32, [64], BF16)
                NP = 2
                pgf_full = sb([128, 4, 160], F32, phS)
                pg_f = [Tl(pgf_full.t[:, 0:NP, :]), Tl(pgf_full.t[:, NP:2 * NP, :])]
                pg_b = [carve(128, [4, 160], BF16) for _ in range(2)]
                kT_pg = [carve(128, [4, 128], BF16) for _ in range(2)]
                kpT_pg = [carve(32, [4, 128], BF16) for _ in range(2)]
                Pt_s = [carve(128, [256], BF16) for _ in range(2)]
                rdn = sb([128, 64], F32, phS)
                olat = carve(128, [8, 128], BF16)
                memset("pool", olat[:], 0.0)
                mk_sb = carve(128, [2, 256], BF16)
                mkTS = carve(64, [4, 256], BF16)
                mvS_aug = carve(128, [2, 4, 65], BF16)
                memset("pool", mvS_aug[:], 1.0)
                Pm = [carve(128, [8, 128], BF16) for _ in range(4)]
                for t_ in Pm:
                    memset("pool", t_[:], 0.0)
                rd4 = sb([128, 4], F32, phS)
                zero_b = carve(128, [512], BF16)
                memset("pool", zero_b[:], 0.0)
                ck_v = D["cache_ckv"]
                kp_v = D["cache_kpe"]
                for tau in range(4):
                    q_tile(D["x_smp"][tau * 128:(tau + 1) * 128, :], D["cs_smp"][tau * 128:(tau + 1) * 128, :], rw_smp[:, tau, :], 0)
                    for hh in range(2):
                        pt_ = next_tr()
                        for h4 in range(4):
                            tr(pt_[0:32, h4 * 128:(h4 + 1) * 128], q_b[:, hh * 4 + h4, 64:96], ident_b[:], last=(h4 == 3))
                        cp(evac_eng(), Vw(qpeT.t[:, hh * 4:(hh + 1) * 4, :], qpeT.b),
                           Vw(pt_.t[0:32, 0:512].rearrange("p (h t) -> p h t", h=4), pt_.b))
                    for hh in range(2):
                        p = next_mm()
                        for h4 in range(4):
                            h = hh * 4 + h4
                            mm(p[:, h4 * 128:(h4 + 1) * 128], WukT[:, h, :], qT[0:64, h, 0:128], last=(h4 == 3))
                        cp(evac_eng(), Vw(qlat_all.t[:, hh * 4:(hh + 1) * 4, :], qlat_all.b),
                           Vw(p.t[:, :].rearrange("p (h t) -> p h t", h=4), p.b))
                    for q in range(4):
                        sq = 4 * tau + q
                        cols = slice(32 * q, 32 * q + 8)
                        cp("dve", Vw(qlat_s.t[:].rearrange("p (h t) -> p h t", h=8), qlat_s.b), Vw(qlat_all.t[:, :, cols], qlat_all.b))
                        cp("pool", Vw(qpe_s.t[:].rearrange("p (h t) -> p h t", h=8), qpe_s.b), Vw(qpeT.t[:, :, cols], qpeT.b))
                        acc = ps_acc[0]
                        ix = idx_t[sq % 2]
                        dma("sp", ptb[:], pt_rows[sq, :].partition_broadcast(128))
                        cp("dve", ptf[:], ptb[:])
                        ts("dve", ptf[:], ptf[:], 128.0, io_t[:, 0:1], ALU.mult, ALU.add)
                        cp("dve", ix[:], ptf[:])
                        mm(acc[:, 0:128], zero_b[:, 0:128], zero_b[:, 0:128], start=True, stop=False, last=True)
                        for step in range(64 // NP):
                            b_ = step % 2
                            for k in range(NP):
                                pgi = step * NP + k
                                S.dma_gather(pg_f[b_].t[:, k, 0:128], ck_v, ix.t[:, pgi:pgi + 1], reads=[ix.b], writes=[pg_f[b_].b])
                                S.dma_gather(pg_f[b_].t[:, k, 128:160], kp_v, ix.t[:, pgi:pgi + 1], reads=[ix.b], writes=[pg_f[b_].b])
                            cp("dve", pg_b[b_][:, 0:NP, :], pg_f[b_][:])
                            pt_ = next_tr()
                            for k in range(NP):
                                tr(pt_[:, k * 128:(k + 1) * 128], pg_b[b_][:, k, 0:128], ident_b[:], last=False)
                            for k in range(NP):
                                tr(pt_[0:32, 512 + k * 128:512 + (k + 1) * 128], pg_b[b_][:, k, 128:160], ident_b[:], last=(k == NP - 1))
                            cp("act", kT_pg[b_][:, 0:NP, :], Vw(pt_.t[:, 0:NP * 128].rearrange("p (k t) -> p k t", k=NP), pt_.b))
                            cp("dve", kpT_pg[b_][:, 0:NP, :], Vw(pt_.t[0:32, 512:512 + NP * 128].rearrange("p (k t) -> p k t", k=NP), pt_.b))
                            p = next_mm()
                            for k in range(NP):
                                mm(p[:, k * 64:(k + 1) * 64], kT_pg[b_][:, k, :], qlat_s[:], start=True, stop=False, last=False)
                                mm(p[:, k * 64:(k + 1) * 64], kpT_pg[b_][:, k, :], qpe_s[:], start=False, stop=True, last=(k == NP - 1))
                            act(Pt_s[b_][:, 0:NP * 64], p[:, 0:NP * 64], AF.Exp, scale=MLA_SCALE)
                            for k in range(NP):
                                mm(acc[:, 0:64], pg_b[b_][:, k, 0:128], Pt_s[b_][:, k * 64:(k + 1) * 64], start=False, stop=False, last=False)
                                mm(acc[:, 64:128], ones_b[:], Pt_s[b_][:, k * 64:(k + 1) * 64], start=False, stop=False, last=(k == NP - 1))
                        p = next_mm()
                        mm(p[:, 0:64], ckvTS[:, tau, :], qlat_s[:], start=True, stop=False, last=False)
                        mm(p[:, 0:64], kpeTS[:, tau, :], qpe_s[:], start=False, stop=False, last=False)
                        mm(p[:, 0:64], ident_b[:], msk_b[:, q, :], start=False, stop=True, last=True)
                        act(Pt_s[0][:, 0:64], p[:, 0:64], AF.Exp, scale=MLA_SCALE)
                        mm(acc[:, 0:64], ckvS_b[:, tau, :], Pt_s[0][:, 0:64], start=False, stop=True, last=False)
                        mm(acc[:, 64:128], ones_b[:], Pt_s[0][:, 0:64], start=False, stop=True, last=True)
                        recip(rdn[:], acc[:, 64:128])
                        tt("dve", Vw(olat.t[:, :, cols], olat.b), Vw(acc.t[:, 0:64].rearrange("p (h t) -> p h t", h=8), acc.b),
                           Vw(rdn.t[:].rearrange("p (h t) -> p h t", h=8), rdn.b), ALU.mult)
                    po = next_mm()
                    for h in range(8):
                        mm(po[:, h * 64:(h + 1) * 64], olat[:, h, :], W_uv[:, h * 64:(h + 1) * 64], last=(h == 7))
                    tt("dve", mixed[:, 0, 0:512], po[:, :], gate[:, 0, 0:512], ALU.mult)
                    pm = ps_acc[1]
                    mm(pm[:, 0:260], zero_b[:, 0:128], zero_b[:, 0:260], start=True, stop=False, last=True)
                    for q in range(4):
                        sq = 4 * tau + q
                        cols = slice(32 * q, 32 * q + 8)
                        for j in range(2):
                            dma("sp", mk_sf[:], D["cache_mem_k"][sq, j * 128:(j + 1) * 128, :])
                            cp("pool", mk_sb[:, j, :], mk_sf[:])
                            dma("sp", mk_sf[:], D["cache_mem_v"][sq, j * 128:(j + 1) * 128, :])
                            cp("pool", Vw(mvS_aug.t[:, j, :, 0:64], mvS_aug.b), Vw(mk_sf.t[:].rearrange("p (h d) -> p h d", h=4), mk_sf.b))
                        for j in range(2):
                            pt_ = next_tr()
                            for h in range(4):
                                tr(pt_[0:64, h * 128:(h + 1) * 128], mk_sb[:, j, h * 64:(h + 1) * 64], ident_b[:], last=(h == 3))
                            cp(evac_eng(), Vw(mkTS.t[:, :, j * 128:(j + 1) * 128], mkTS.b),
                               Vw(pt_.t[0:64, 0:512].rearrange("p (h t) -> p h t", h=4), pt_.b))
                        p = next_mm()
                        for j in range(2):
                            for h in range(4):
                                c0 = (j * 4 + h) * 8
                                mm(p[:, c0:c0 + 8], mkTS[:, h, j * 128:(j + 1) * 128], qxT[:, h, cols], last=(j == 1 and h == 3))
                        act(Vw(Pm[q].t[:, :, cols], Pm[q].b), Vw(p.t[:, 0:64].rearrange("p (g t) -> p g t", g=8), p.b), AF.Exp, scale=X_SCALE)
                        for h in range(4):
                            for j in range(2):
                                mm(pm[:, h * 65:(h + 1) * 65], Pm[q][:, j * 4 + h, :], mvS_aug[:, j, h, :],
                                   start=False, stop=(q == 3 and j == 1), last=(h == 3 and j == 1))
                    S.op("dve", lambda e: e.tensor_scalar(out=rd4.t[:, 0:4], in0=pm.t[:, 0:260].rearrange("p (h d) -> p h d", h=4)[:, :, 64],
                                                          scalar1=1e-30, scalar2=None, op0=ALU.add),
                         reads=[pm.b], writes=[rd4.b])
                    recip(rd4[:], rd4[:])
                    for h in range(4):
                        stt("dve", mixed[:, 0, 768 + h * 64:768 + (h + 1) * 64], pm[:, h * 65:h * 65 + 64], rd4[:, h:h + 1],
                            gate[:, 0, 768 + h * 64:768 + (h + 1) * 64], ALU.mult, ALU.mult)
                    out_tile(D["x_smp"][tau * 128:(tau + 1) * 128, :], 0,
                             [(O["y_smp"][(4 * tau + q) * 8:(4 * tau + q + 1) * 8, :], 32 * q, 32 * q + 8) for q in range(4)])
                S.barrier()
                phS.close()
            S.barrier()
            phB2.close()
            phB.close()
        S.finish()
    return nc


_NC_CACHE = {}


def _rope_table(pos):
    half = 16
    inv = (1.0 / (np.float32(10000.0) ** (np.arange(half, dtype=np.float32) * np.float32(2.0 / 32)))).astype(np.float32)
    ang = pos.astype(np.float32)[:, None] * inv[None, :]
    c = np.cos(ang).astype(np.float32)
    s = np.sin(ang).astype(np.float32)
    return np.concatenate([c, c, s, s], axis=1).astype(np.float32)


def _tri_consts(C):
    idx = np.arange(128)
    same = (idx[:, None] // C) == (idx[None, :] // C)
    s_le_t = (idx[:, None] <= idx[None, :]) & same
    s_lt_t = (idx[:, None] < idx[None, :]) & same
    s_gt_t = (idx[:, None] > idx[None, :]) & same
    return np.concatenate([s_le_t, same, s_lt_t, s_le_t, s_gt_t], axis=1).astype(np.float32)


def kernel(**inputs):
    f = lambda a: np.ascontiguousarray(np.asarray(a))
    inp = {k: f(v) for k, v in inputs.items()}
    if "nc" not in _NC_CACHE:
        _NC_CACHE["nc"] = build_nc()
    nc = _NC_CACHE["nc"]
    xp = inp["x_prompt"]
    xs = inp["x_sample"].reshape(128 * 8, 1024)
    ident = np.eye(128, dtype=np.float32)
    idx = np.arange(128)
    tri_mask = np.where(idx[:, None] > idx[None, :], NEG, 0.0).astype(np.float32)
    full_mask = np.full((128, 128), NEG, np.float32)
    zero_mask = np.zeros((128, 128), np.float32)
    cs_all = _rope_table(np.arange(4096))
    cs8 = _rope_table(8192 + np.arange(8))
    cs_smp = np.zeros((16, 32, 64), np.float32)
    cs_smp[:, 0:8] = cs8[None]
    cs_smp = cs_smp.reshape(512, 64)
    r_ = np.arange(128)
    ms = np.full((128, 4, 8, 8), NEG, np.float32)
    for q_ in range(4):
        for t_ in range(8):
            ok = (r_ // 32 == q_) & (r_ % 32 <= t_) & (r_ % 32 < 8)
            ms[ok, q_, :, t_] = 0.0
    masks_s = ms.reshape(128, 256)
    shared = dict(
        ident=ident, tri64=_tri_consts(64), tri8=_tri_consts(8),
        cache_ckv=inp["cache_ckv"].reshape(10240 * 128, 128)[:NPOOL * 128], cache_kpe=inp["cache_kpe"].reshape(10240 * 128, 32)[:NPOOL * 128],
        iota_p=np.arange(128, dtype=np.float32).reshape(128, 1),
        ln_g=f(inp["ln_g"].reshape(8, 128).T), w_in=inp["w_in"].reshape(1024, 2592), q_norm_g=f(inp["q_norm_g"].reshape(2, 128).T),
        kv_norm_g=inp["kv_norm_g"].reshape(128), w_uq=inp["w_uq"].reshape(256, 768), w_uk=inp["w_uk"].reshape(128, 512),
        w_uv=inp["w_uv"].reshape(128, 512), shift_mu=inp["shift_mu"].reshape(896), w0=inp["w0"].reshape(256),
        w_up=inp["w_up"].reshape(64, 256), a0=inp["a0"].reshape(256), a_up=inp["a_up"].reshape(64, 256),
        k_k=inp["k_k"].reshape(256), k_a=inp["k_a"].reshape(256), r_k=inp["r_k"].reshape(256),
        lnx_g=inp["lnx_g"].reshape(256), lnx_b=inp["lnx_b"].reshape(256), mem_norm_g=f(inp["mem_norm_g"].reshape(8, 128).T),
        w_mem_kv=inp["w_mem_kv"].reshape(1024, 512), w_out=inp["w_out"].reshape(1024, 1024),
        final_g=inp["final_g"].reshape(1024), cs_seq=cs_all, cs_smp=cs_smp, masks_s=masks_s,
    )
    in_maps = []
    for c in range(8):
        b, j = c // 2, c % 2
        xb = xp[b].reshape(32, 128, 1024)
        m = dict(shared)
        m["x_seq"] = xp[b]
        m["x_own"] = f(xb[j::2].reshape(2048, 1024))
        xpad = np.zeros((16, 32, 1024), np.float32)
        xpad[:, 0:8] = xs[c * 128:(c + 1) * 128].reshape(16, 8, 1024)
        m["x_smp"] = xpad.reshape(512, 1024)
        m["mem"] = inp["mem_prompt"][b]
        m["cs_own"] = f(cs_all.reshape(32, 128, 64)[j::2].reshape(2048, 64))
        m["masks"] = f(np.concatenate([tri_mask, full_mask] if j == 0 else [zero_mask, tri_mask], axis=1))
        m["sel"] = f(np.tile(np.array([[1.0, 0.0]] if j == 0 else [[0.0, 1.0]], np.float32), (128, 1)))
        pt_c = inp["page_table"][c * 16:(c + 1) * 16].reshape(1, 1024).astype(np.int32)
        m["page_table"] = f(pt_c % NPOOL) if KDEV else f(pt_c)
        m["state_wkv"] = f(inp["state_wkv"][0, c * 16:(c + 1) * 16])
        m["state_shift"] = f(inp["state_shift"][0, c * 16:(c + 1) * 16])
        m["cache_mem_k"] = f(inp["cache_mem_k"][0, c * 16:(c + 1) * 16].reshape(16, 256, 256))
        m["cache_mem_v"] = f(inp["cache_mem_v"][0, c * 16:(c + 1) * 16].reshape(16, 256, 256))
        in_maps.append(m)
    res = run_bass_kernel_spmd(nc, in_maps, core_ids=list(range(8)))
    R = res.results
    y_prompt = np.zeros((4, 32, 128, 1024), np.float32)
    for c in range(8):
        y_prompt[c // 2, (c % 2)::2] = R[c]["y_own"].reshape(16, 128, 1024)
    y_prompt = y_prompt.reshape(4, 4096, 1024)
    y_sample = np.concatenate([R[c]["y_smp"] for c in range(8)], 0).reshape(128, 8, 1024)
    ev = [R[2 * b] for b in range(4)]
    ckv_p = np.stack([r["ckv_seq"] for r in ev])[None]
    kpe_p = np.stack([r["kpe_seq"] for r in ev])[None]
    wkv_p = np.stack([r["wkv_p"] for r in ev])[None]
    shift_p = np.stack([r["shift_p"].reshape(896) for r in ev])[None]
    memk = np.stack([r["memk"].reshape(256, 4, 64) for r in ev])[None]
    memv = np.stack([r["memv"].reshape(256, 4, 64) for r in ev])[None]
    ckv_s = np.concatenate([R[c]["ckv_s"] for c in range(8)], 0).reshape(1, 128, 8, 128)
    kpe_s = np.concatenate([R[c]["kpe_s"] for c in range(8)], 0).reshape(1, 128, 8, 32)
    wkv_s = np.concatenate([R[c]["wkv_s"] for c in range(8)], 0)[None]
    shift_s = np.concatenate([R[c]["shift_s"] for c in range(8)], 0)[None]
    outs = (y_prompt, y_sample, ckv_p, kpe_p, wkv_p, shift_p, memk, memv, ckv_s, kpe_s, wkv_s, shift_s)
    return tuple(np.ascontiguousarray(o, dtype=np.float32) for o in outs)
```
